# Optimizing a Trainium2 kernel written in Bass

```python
import math
import jax, jax.numpy as jnp
from jax import lax
import numpy as np

D_MODEL = 1024
BATCH = 8
SEQ = 4096
DEPTH = 2

N_MIXERS = 4
MIX_WIDTH = D_MODEL
GROUP_WIDTH = MIX_WIDTH // N_MIXERS
S5_WIDTH = GROUP_WIDTH
S5_GROUP_CH = 16
S5_GROUPS = S5_WIDTH // S5_GROUP_CH
S5_STATE = 64
CONV_WIDTH = GROUP_WIDTH
CONV_K = 31
MLA_HEADS = 4
MLA_V_DIM = GROUP_WIDTH // MLA_HEADS
MLA_NOPE_DIM = MLA_V_DIM
MLA_ROPE_DIM = MLA_NOPE_DIM // 2
MLA_Q_RANK = GROUP_WIDTH
MLA_KV_RANK = GROUP_WIDTH // 2
GQA_HEADS = 4
GQA_KV_HEADS = 2
GQA_HEAD_DIM = GROUP_WIDTH // GQA_HEADS
D_FF = ((8 * D_MODEL // 3 + 127) // 128) * 128
FFN_CONV_K = 3
GRID_W = 64
Q_BLOCK = 128
ROPE_THETA = 10000.0
NORM_EPS = 1e-6

IN_SIZES = (S5_WIDTH, 2 * CONV_WIDTH, MLA_Q_RANK, MLA_KV_RANK, MLA_ROPE_DIM,
            GQA_HEADS * GQA_HEAD_DIM, GQA_KV_HEADS * GQA_HEAD_DIM, GQA_KV_HEADS * GQA_HEAD_DIM)
IN_COLS = sum(IN_SIZES)
IN_SPLIT_POINTS = tuple(int(v) for v in np.cumsum(IN_SIZES)[:-1])

kernel_name = 'hymba_style_s5_conformer_mla_gqa_encoder'


def rms_norm(x, g):
    xf = x.astype(jnp.float32)
    y = xf * lax.rsqrt(jnp.mean(xf * xf, axis=-1, keepdims=True) + NORM_EPS)
    return (y * g.astype(jnp.float32)).astype(x.dtype)


def layer_norm(x, g, b):
    xf = x.astype(jnp.float32)
    xc = xf - jnp.mean(xf, axis=-1, keepdims=True)
    y = xc * lax.rsqrt(jnp.mean(xc * xc, axis=-1, keepdims=True) + NORM_EPS)
    return (y * g.astype(jnp.float32) + b.astype(jnp.float32)).astype(x.dtype)


def depthwise_conv(x, w, b):
    k, c = w.shape
    pad = k // 2
    y = lax.conv_general_dilated(x, w[:, None, :].astype(x.dtype), (1,), [(pad, pad)],
                                 dimension_numbers=('NWC', 'WIO', 'NWC'),
                                 feature_group_count=c)
    return y + b.astype(x.dtype)


def axial_rope_tables(rows, rot_dim, dtype):
    m = rot_dim // 4
    inv_freq = ROPE_THETA ** (-jnp.arange(m, dtype=jnp.float32) / m)
    row = jnp.repeat(jnp.arange(rows, dtype=jnp.float32), GRID_W)
    col = jnp.tile(jnp.arange(GRID_W, dtype=jnp.float32), rows)
    ang_r = (row[:, None] * inv_freq)[:, None, :]
    ang_c = (col[:, None] * inv_freq)[:, None, :]
    return (jnp.cos(ang_r).astype(dtype), jnp.sin(ang_r).astype(dtype),
            jnp.cos(ang_c).astype(dtype), jnp.sin(ang_c).astype(dtype))


def _rotate_half_split(x, cos, sin):
    x1, x2 = jnp.split(x, 2, axis=-1)
    return jnp.concatenate([x1 * cos - x2 * sin, x1 * sin + x2 * cos], axis=-1)


def axial_rope(x, tables):
    cos_r, sin_r, cos_c, sin_c = tables
    x_row, x_col = jnp.split(x, 2, axis=-1)
    return jnp.concatenate([_rotate_half_split(x_row, cos_r, sin_r),
                            _rotate_half_split(x_col, cos_c, sin_c)], axis=-1)


def block_attention(q, k, v):
    b, l, hk, g, dk = q.shape
    dv = v.shape[-1]
    nblk = l // Q_BLOCK
    scale = dk ** -0.5
    qb = q.reshape(b, nblk, Q_BLOCK, hk, g, dk).transpose(1, 0, 2, 3, 4, 5)

    def one_block(q_blk):
        s = jnp.einsum('bqhgd,bkhd->bhgqk', q_blk, k).astype(jnp.float32) * scale
        p = jax.nn.softmax(s, axis=-1).astype(v.dtype)
        return jnp.einsum('bhgqk,bkhd->bqhgd', p, v)

    o = lax.map(one_block, qb)
    return o.transpose(1, 0, 2, 3, 4, 5).reshape(b, l, hk * g * dv)


def _complex_affine_combine(earlier, later):
    a1r, a1i, b1r, b1i = earlier
    a2r, a2i, b2r, b2i = later
    return (a2r * a1r - a2i * a1i, a2r * a1i + a2i * a1r,
            a2r * b1r - a2i * b1i + b2r, a2r * b1i + a2i * b1r + b2i)


def s5_mixer(u, lam_re, lam_im, log_step, b_re, b_im, c_re, c_im, d_skip, w_glu, b_glu):
    bsz, seq_len, _ = u.shape
    dt = u.dtype
    ug = u.reshape(bsz, seq_len, S5_GROUPS, S5_GROUP_CH)
    y = d_skip * u
    for direction, rev in ((0, False), (1, True)):
        lr = lam_re[direction].astype(jnp.float32)
        li = lam_im[direction].astype(jnp.float32)
        step = jnp.exp(log_step[direction].astype(jnp.float32))[:, None]
        mag = jnp.exp(lr * step)
        ang = li * step
        ar = mag * jnp.cos(ang)
        ai = mag * jnp.sin(ang)
        den = lr * lr + li * li
        fr = ((ar - 1.0) * lr + ai * li) / den
        fi = (ai * lr - (ar - 1.0) * li) / den
        br = b_re[direction].astype(jnp.float32)
        bi = b_im[direction].astype(jnp.float32)
        bbr = (fr[..., None] * br - fi[..., None] * bi).astype(dt)
        bbi = (fr[..., None] * bi + fi[..., None] * br).astype(dt)
        xr = jnp.einsum('blgh,gph->blgp', ug, bbr)
        xi = jnp.einsum('blgh,gph->blgp', ug, bbi)
        a_shape = (1, seq_len, S5_GROUPS, S5_STATE)
        ar_t = jnp.broadcast_to(ar.astype(dt), a_shape)
        ai_t = jnp.broadcast_to(ai.astype(dt), a_shape)
        _, _, sr, si = lax.associative_scan(_complex_affine_combine, (ar_t, ai_t, xr, xi),
                                            reverse=rev, axis=1)
        y_dir = (jnp.einsum('blgp,ghp->blgh', sr, c_re[direction])
                 - jnp.einsum('blgp,ghp->blgh', si, c_im[direction]))
        y = y + y_dir.reshape(bsz, seq_len, S5_WIDTH)
    y = jax.nn.gelu(y)
    return y * jax.nn.sigmoid(y @ w_glu + b_glu)


def conformer_conv(h, dw_w, dw_b, ln_g, ln_b):
    val, gate = jnp.split(h, 2, axis=-1)
    g = val * jax.nn.sigmoid(gate)
    g = depthwise_conv(g, dw_w, dw_b)
    return jax.nn.silu(layer_norm(g, ln_g, ln_b))


def mla_mixer(q_lat, kv_lat, k_pe, q_norm_g, w_uq, kv_norm_g, w_ukv, rope):
    b, l, _ = q_lat.shape
    q = (rms_norm(q_lat, q_norm_g) @ w_uq).reshape(b, l, MLA_HEADS, MLA_NOPE_DIM + MLA_ROPE_DIM)
    q_nope, q_pe = jnp.split(q, [MLA_NOPE_DIM], axis=-1)
    q = jnp.concatenate([q_nope, axial_rope(q_pe, rope)], axis=-1)
    kv = (rms_norm(kv_lat, kv_norm_g) @ w_ukv).reshape(b, l, MLA_HEADS, MLA_NOPE_DIM + MLA_V_DIM)
    k_nope, v = jnp.split(kv, [MLA_NOPE_DIM], axis=-1)
    k_pe = axial_rope(k_pe[:, :, None, :], rope)
    k = jnp.concatenate([k_nope, jnp.broadcast_to(k_pe, (b, l, MLA_HEADS, MLA_ROPE_DIM))], axis=-1)
    return block_attention(q[:, :, :, None, :], k, v)


def gqa_mixer(q, k, v, q_norm_g, k_norm_g, rope):
    b, l, _ = q.shape
    q = axial_rope(rms_norm(q.reshape(b, l, GQA_HEADS, GQA_HEAD_DIM), q_norm_g), rope)
    k = axial_rope(rms_norm(k.reshape(b, l, GQA_KV_HEADS, GQA_HEAD_DIM), k_norm_g), rope)
    v = v.reshape(b, l, GQA_KV_HEADS, GQA_HEAD_DIM)
    q = q.reshape(b, l, GQA_KV_HEADS, GQA_HEADS // GQA_KV_HEADS, GQA_HEAD_DIM)
    return block_attention(q, k, v)


def conv_ffn(h, w_up, conv_w, conv_b, w_down):
    u = depthwise_conv(h @ w_up, conv_w, conv_b)
    gate, up = jnp.split(u, 2, axis=-1)
    return (jax.nn.silu(gate) * up) @ w_down


def setup_inputs(seed: int = 0) -> dict:
    key = jax.random.key(seed)
    keys = iter(jax.random.split(key, 40))

    def nrm(shape, std):
        return std * jax.random.normal(next(keys), shape, jnp.float32)

    def gain(shape):
        return 1.0 + nrm(shape, 0.02)

    nl = DEPTH
    s5_shape = (nl, 2, S5_GROUPS, S5_STATE)
    n_idx = jnp.arange(S5_STATE, dtype=jnp.float32)
    return {
        'x': nrm((BATCH, SEQ, D_MODEL), 1.0),
        'attn_norm_g': gain((nl, D_MODEL)),
        'w_in': nrm((nl, D_MODEL, IN_COLS), D_MODEL ** -0.5),
        's5_lam_re': -0.5 + nrm(s5_shape, 0.02),
        's5_lam_im': math.pi * n_idx + nrm(s5_shape, 0.02),
        's5_log_step': jax.random.uniform(next(keys), (nl, 2, S5_GROUPS), jnp.float32,
                                          math.log(1e-3), math.log(1e-1)),
        's5_b_re': nrm((nl, 2, S5_GROUPS, S5_STATE, S5_GROUP_CH), S5_GROUP_CH ** -0.5),
        's5_b_im': nrm((nl, 2, S5_GROUPS, S5_STATE, S5_GROUP_CH), S5_GROUP_CH ** -0.5),
        's5_c_re': nrm((nl, 2, S5_GROUPS, S5_GROUP_CH, S5_STATE), 0.5),
        's5_c_im': nrm((nl, 2, S5_GROUPS, S5_GROUP_CH, S5_STATE), 0.5),
        's5_d': nrm((nl, S5_WIDTH), 1.0),
        's5_w_glu': nrm((nl, S5_WIDTH, S5_WIDTH), S5_WIDTH ** -0.5),
        's5_b_glu': nrm((nl, S5_WIDTH), 0.02),
        'conv_dw_w': nrm((nl, CONV_K, CONV_WIDTH), CONV_K ** -0.5),
        'conv_dw_b': nrm((nl, CONV_WIDTH), 0.02),
        'conv_ln_g': gain((nl, CONV_WIDTH)),
        'conv_ln_b': nrm((nl, CONV_WIDTH), 0.02),
        'mla_q_norm_g': gain((nl, MLA_Q_RANK)),
        'mla_w_uq': nrm((nl, MLA_Q_RANK, MLA_HEADS * (MLA_NOPE_DIM + MLA_ROPE_DIM)), MLA_Q_RANK ** -0.5),
        'mla_kv_norm_g': gain((nl, MLA_KV_RANK)),
        'mla_w_ukv': nrm((nl, MLA_KV_RANK, MLA_HEADS * (MLA_NOPE_DIM + MLA_V_DIM)), MLA_KV_RANK ** -0.5),
        'gqa_q_norm_g': gain((nl, GQA_HEAD_DIM)),
        'gqa_k_norm_g': gain((nl, GQA_HEAD_DIM)),
        'mix_norm_g': gain((nl, N_MIXERS, GROUP_WIDTH)),
        'w_out': nrm((nl, MIX_WIDTH, D_MODEL), MIX_WIDTH ** -0.5),
        'ffn_norm_g': gain((nl, D_MODEL)),
        'ffn_w_up': nrm((nl, D_MODEL, 2 * D_FF), D_MODEL ** -0.5),
        'ffn_conv_w': nrm((nl, FFN_CONV_K, 2 * D_FF), FFN_CONV_K ** -0.5),
        'ffn_conv_b': nrm((nl, 2 * D_FF), 0.02),
        'ffn_w_down': nrm((nl, D_FF, D_MODEL), D_FF ** -0.5),
        'final_norm_g': gain((D_MODEL,)),
    }


def reference(x, attn_norm_g, w_in, s5_lam_re, s5_lam_im, s5_log_step, s5_b_re, s5_b_im,
              s5_c_re, s5_c_im, s5_d, s5_w_glu, s5_b_glu, conv_dw_w, conv_dw_b, conv_ln_g,
              conv_ln_b, mla_q_norm_g, mla_w_uq, mla_kv_norm_g, mla_w_ukv, gqa_q_norm_g,
              gqa_k_norm_g, mix_norm_g, w_out, ffn_norm_g, ffn_w_up, ffn_conv_w, ffn_conv_b,
              ffn_w_down, final_norm_g):
    bsz, seq_len, _ = x.shape
    rows = seq_len // GRID_W
    rope_gqa = axial_rope_tables(rows, GQA_HEAD_DIM, x.dtype)
    rope_mla = axial_rope_tables(rows, MLA_ROPE_DIM, x.dtype)
    for i in range(DEPTH):
        h = rms_norm(x, attn_norm_g[i])
        proj = h @ w_in[i]
        u_s5, conv_in, q_lat, kv_lat, k_pe, q_g, k_g, v_g = jnp.split(proj, IN_SPLIT_POINTS, axis=-1)
        y_s5 = s5_mixer(u_s5, s5_lam_re[i], s5_lam_im[i], s5_log_step[i], s5_b_re[i], s5_b_im[i],
                        s5_c_re[i], s5_c_im[i], s5_d[i], s5_w_glu[i], s5_b_glu[i])
        y_conv = conformer_conv(conv_in, conv_dw_w[i], conv_dw_b[i], conv_ln_g[i], conv_ln_b[i])
        y_mla = mla_mixer(q_lat, kv_lat, k_pe, mla_q_norm_g[i], mla_w_uq[i], mla_kv_norm_g[i],
                          mla_w_ukv[i], rope_mla)
        y_gqa = gqa_mixer(q_g, k_g, v_g, gqa_q_norm_g[i], gqa_k_norm_g[i], rope_gqa)
        mixed = jnp.stack([y_s5, y_conv, y_mla, y_gqa], axis=2)
        mixed = rms_norm(mixed, mix_norm_g[i]).reshape(bsz, seq_len, MIX_WIDTH)
        x = x + mixed @ w_out[i]
        h = rms_norm(x, ffn_norm_g[i])
        x = x + conv_ffn(h, ffn_w_up[i], ffn_conv_w[i], ffn_conv_b[i], ffn_w_down[i])
    return rms_norm(x, final_norm_g)
```

```python
import contextlib
import math
import numpy as np
import concourse.bass as bass
import concourse.mybir as mybir
from concourse.bass_utils import run_bass_kernel_spmd

F32 = mybir.dt.float32
BF16 = mybir.dt.bfloat16
AF = mybir.ActivationFunctionType
ALU = mybir.AluOpType

L = 4096
D = 1024
NB = 8
TB = 512
DEPTH = 2
DFF = 2816
NCOL = 2112
EPS = 1e-6
TWO_PI = 2.0 * math.pi
MAGIC = 12582912.0


class Tl:
    def __init__(self, h):
        self.h = h
        self.w = {}
        self.r = {}

    def __getitem__(self, k):
        return self.h[k]


class TlV(Tl):
    def __init__(self, h, c0, n):
        self.h = h
        self.c0 = c0
        self.n = n
        self.w = {}
        self.r = {}

    def __getitem__(self, k):
        if not isinstance(k, tuple):
            k = (k, slice(None))
        rows, cols = k
        a, b, stp = cols.indices(self.n)
        assert stp == 1
        return self.h[rows, self.c0 + a:self.c0 + b]


class KB:
    def __init__(self, nc):
        self.nc = nc
        self.eng = {"pe": nc.tensor, "act": nc.scalar, "dve": nc.vector, "pool": nc.gpsimd, "sp": nc.sync}
        self.sem = {}
        self.cnt = {}
        self.seen = {e: {} for e in self.eng}
        self.stack = contextlib.ExitStack()
        for e in ("pe", "act", "dve", "pool"):
            self.sem[e] = self.stack.enter_context(nc.semaphore("prog_" + e))
            self.cnt[e] = 0
        self.dsem = [self.stack.enter_context(nc.semaphore(f"dma{i}")) for i in range(16)]
        self.dcnt = [0] * 16
        self.dnext = {"sp": 0, "pool": 0, "act": 0}
        self.uid = 0

    def name(self, p):
        self.uid += 1
        return f"{p}_{self.uid}"

    def sb(self, st, shape, dt, nm="t"):
        return Tl(st.enter_context(self.nc.sbuf_tensor(self.name(nm), list(shape), dt)))

    def ps(self, st, shape, dt=F32, nm="p"):
        return Tl(st.enter_context(self.nc.psum_tensor(self.name(nm), list(shape), dt)))

    def wait(self, e, toks):
        seen = self.seen[e]
        need = {}
        for (s, v) in toks:
            if seen.get(s, 0) < v and need.get(s, 0) < v:
                need[s] = v
        for s, v in need.items():
            self.eng[e].wait_ge(s, v)
            seen[s] = v

    def deps(self, reads, writes, part=False):
        toks = []
        for t in reads:
            toks.extend(t.w.items())
        for t in writes:
            if not part:
                toks.extend(t.w.items())
            toks.extend(t.r.items())
        return toks

    def fin(self, tok, reads, writes, part=False):
        for t in reads:
            if t.r.get(tok[0], 0) < tok[1]:
                t.r[tok[0]] = tok[1]
        for t in writes:
            if part:
                if t.w.get(tok[0], 0) < tok[1]:
                    t.w[tok[0]] = tok[1]
            else:
                t.w = {tok[0]: tok[1]}
                t.r = {}

    def op(self, e, reads, writes, fn):
        self.wait(e, self.deps(reads, writes))
        inst = fn()
        self.cnt[e] += 1
        inst.then_inc(self.sem[e], 1)
        tok = (self.sem[e], self.cnt[e])
        self.fin(tok, reads, writes)
        return tok

    def dma(self, q, reads, writes, out, in_, part=False):
        self.wait(q, self.deps(reads, writes, part))
        base, npool = {"sp": (0, 6), "pool": (6, 6), "act": (12, 4)}[q]
        i = base + self.dnext[q]
        self.dnext[q] = (self.dnext[q] + 1) % npool
        inst = self.eng[q].dma_start(out=out, in_=in_)
        self.dcnt[i] += 16
        inst.then_inc(self.dsem[i], 16)
        tok = (self.dsem[i], self.dcnt[i])
        self.fin(tok, reads, writes, part)
        return tok

    def mmg(self, out_t, out_ap, terms, reads):
        self.wait("pe", self.deps(reads, [out_t]))
        n = len(terms)
        inst = None
        for i, (l, r) in enumerate(terms):
            inst = self.nc.tensor.matmul(out_ap, l, r, start=(i == 0), stop=(i == n - 1))
        self.cnt["pe"] += 1
        inst.then_inc(self.sem["pe"], 1)
        tok = (self.sem["pe"], self.cnt["pe"])
        self.fin(tok, reads, [out_t])
        return tok

    def mm_acc(self, out_t, out_ap, l, r, reads, start, stop, first):
        d = self.deps(reads, [out_t] if first else [])
        self.wait("pe", d)
        inst = self.nc.tensor.matmul(out_ap, l, r, start=start, stop=stop)
        self.cnt["pe"] += 1
        inst.then_inc(self.sem["pe"], 1)
        tok = (self.sem["pe"], self.cnt["pe"])
        for t in reads:
            if t.r.get(tok[0], 0) < tok[1]:
                t.r[tok[0]] = tok[1]
        if first:
            out_t.r = {}
        out_t.w = {tok[0]: tok[1]}
        return tok

    def phase_end(self):
        nc = self.nc
        toks = [(self.dsem[i], self.dcnt[i]) for i in range(len(self.dsem)) if self.dcnt[i] > 0]
        toks += [(self.sem[e], self.cnt[e]) for e in self.sem if self.cnt[e] > 0]
        for e in self.eng:
            self.wait(e, toks)
        nc.all_engine_barrier()

    def act(self, reads, writes, out, in_, func, scale=1.0, bias=None):
        if bias is None:
            return self.op("act", reads, writes, lambda: self.nc.scalar.activation(out=out, in_=in_, func=func, scale=scale))
        return self.op("act", reads, writes,
                       lambda: self.nc.scalar.activation(out=out, in_=in_, func=func, scale=scale, bias=bias))

    def tt(self, reads, writes, out, a, b, op, e="dve"):
        return self.op(e, reads, writes, lambda: self.eng[e].tensor_tensor(out=out, in0=a, in1=b, op=op))

    def ts(self, reads, writes, out, a, s1, op0, s2=None, op1=None, e="dve"):
        if op1 is None:
            return self.op(e, reads, writes, lambda: self.eng[e].tensor_scalar(out=out, in0=a, scalar1=s1, scalar2=None, op0=op0))
        return self.op(e, reads, writes,
                       lambda: self.eng[e].tensor_scalar(out=out, in0=a, scalar1=s1, scalar2=s2, op0=op0, op1=op1))

    def stt(self, reads, writes, out, a, s, b, op0, op1):
        return self.op("dve", reads, writes,
                       lambda: self.nc.vector.scalar_tensor_tensor(out=out, in0=a, scalar=s, in1=b, op0=op0, op1=op1))

    def recip(self, reads, writes, out, in_):
        return self.op("dve", reads, writes, lambda: self.nc.vector.reciprocal(out=out, in_=in_))

    def memset(self, writes, ap, val, e="dve"):
        return self.op(e, [], writes, lambda: self.eng[e].memset(ap, val))


def rope_swap_idx(r):
    m = r // 4
    idx = np.arange(r)
    q = idx // m
    return np.where(q % 2 == 0, idx + m, idx - m)


def rope_tables(r):
    m = r // 4
    inv = (10000.0 ** (-np.arange(m, dtype=np.float32) / m)).astype(np.float32)
    t = np.arange(L)
    row = (t // 64).astype(np.float32)
    col = (t % 64).astype(np.float32)
    cos = np.zeros((r, L), np.float32)
    sin = np.zeros((r, L), np.float32)
    for d in range(r):
        qd, f = d // m, d % m
        pos = row if qd < 2 else col
        ang = (pos * inv[f]).astype(np.float32)
        cos[d] = np.cos(ang).astype(np.float32)
        s = np.sin(ang).astype(np.float32)
        sin[d] = -s if qd % 2 == 0 else s
    return cos, sin


def host_prep(inp):
    f = np.float32
    sh = {}
    swg = rope_swap_idx(64)
    swm = rope_swap_idx(32)
    w_in = inp["w_in"]
    cols = []
    cols.append(np.arange(0, 768))
    cols.append(np.arange(768, 1024))
    cols.append(np.arange(1024, 1152))
    qg0 = 1184
    cols.append(np.arange(qg0, qg0 + 256))
    cols.append(np.concatenate([qg0 + h * 64 + swg for h in range(4)]))
    kg0 = 1440
    cols.append(np.arange(kg0, kg0 + 128))
    cols.append(np.concatenate([kg0 + h * 64 + swg for h in range(2)]))
    cols.append(np.arange(1568, 1696))
    cols.append(np.arange(1152, 1184))
    cols.append(1152 + swm)
    cols = np.concatenate(cols)
    assert cols.shape[0] == NCOL
    sh["w_in"] = np.ascontiguousarray(w_in[:, :, cols]).astype(f)
    uq = []
    for h in range(4):
        uq.append(np.arange(h * 96, h * 96 + 64))
    for h in range(4):
        uq.append(h * 96 + 64 + np.arange(32))
    for h in range(4):
        uq.append(h * 96 + 64 + swm)
    uq = np.concatenate(uq)
    sh["w_uq"] = np.ascontiguousarray(inp["mla_w_uq"][:, :, uq]).astype(f)
    ukv = []
    for h in range(4):
        ukv.append(np.arange(h * 128, h * 128 + 64))
    for h in range(4):
        ukv.append(np.arange(h * 128 + 64, h * 128 + 128))
    ukv = np.concatenate(ukv)
    sh["w_ukv"] = np.ascontiguousarray(inp["mla_w_ukv"][:, :, ukv]).astype(f)
    sh["w_glu"] = inp["s5_w_glu"].astype(f)
    sh["w_out"] = inp["w_out"].astype(f)
    sh["w_up"] = inp["ffn_w_up"].astype(f)
    sh["w_down"] = inp["ffn_w_down"].astype(f)

    def pp(v, n):
        return np.ascontiguousarray(v.reshape(n, 128).T)

    vec = []
    for l in range(DEPTH):
        c = []
        c.append(pp(inp["attn_norm_g"][l], 8))
        c.append(pp(inp["mla_q_norm_g"][l], 2))
        c.append(pp(inp["mla_kv_norm_g"][l], 1))
        gq = inp["gqa_q_norm_g"][l]
        gk = inp["gqa_k_norm_g"][l]
        c.append(np.tile(gq, 2)[:, None])
        c.append(np.tile(gq[swg], 2)[:, None])
        c.append(np.tile(gk, 2)[:, None])
        c.append(np.tile(gk[swg], 2)[:, None])
        c.append(pp(inp["mix_norm_g"][l].reshape(-1), 8))
        c.append(pp(inp["ffn_norm_g"][l], 8))
        c.append(pp(inp["s5_d"][l], 2))
        c.append(pp(inp["s5_b_glu"][l], 2))
        c.append(pp(inp["conv_dw_b"][l], 2))
        c.append(pp(inp["conv_ln_g"][l], 2))
        c.append(pp(inp["conv_ln_b"][l], 2))
        c.append(pp(inp["final_norm_g"], 8))
        cw = inp["conv_dw_w"][l]
        c.append(np.ascontiguousarray(cw.reshape(31, 2, 128).transpose(2, 1, 0)).reshape(128, 62))
        fw = inp["ffn_conv_w"][l]
        c.append(np.ascontiguousarray(fw.reshape(3, 44, 128).transpose(2, 1, 0)).reshape(128, 132))
        c.append(pp(inp["ffn_conv_b"][l], 44))
        vec.append(np.concatenate(c, axis=1))
    sh["vec"] = np.stack(vec).astype(f)
    s5c = np.zeros((DEPTH, 128, 5, 2, 2, 64), f)
    s5s = np.zeros((DEPTH, 128, 3, 2, 8), f)
    wc = np.zeros((DEPTH, 128, 2, 2, 8, 128), f)
    for l in range(DEPTH):
        for d in range(2):
            for g in range(16):
                m, g8 = g // 8, g % 8
                q, gl = g // 2, g % 2
                for h in range(16):
                    s5c[l, g8 * 16 + h, 0, m, d] = inp["s5_lam_re"][l, d, g]
                    s5c[l, g8 * 16 + h, 1, m, d] = inp["s5_lam_im"][l, d, g]
                    s5c[l, g8 * 16 + h, 2, m, d] = inp["s5_log_step"][l, d, g]
                    s5c[l, g8 * 16 + h, 3, m, d] = inp["s5_b_re"][l, d, g, :, h]
                    s5c[l, g8 * 16 + h, 4, m, d] = inp["s5_b_im"][l, d, g, :, h]
                s5s[l, gl * 64:(gl + 1) * 64, 0, d, q] = inp["s5_lam_re"][l, d, g]
                s5s[l, gl * 64:(gl + 1) * 64, 1, d, q] = inp["s5_lam_im"][l, d, g]
                s5s[l, gl * 64:(gl + 1) * 64, 2, d, q] = inp["s5_log_step"][l, d, g]
                wc[l, gl * 64:(gl + 1) * 64, 0, d, q, g8 * 16:(g8 + 1) * 16] = inp["s5_c_re"][l, d, g].T
                wc[l, gl * 64:(gl + 1) * 64, 1, d, q, g8 * 16:(g8 + 1) * 16] = inp["s5_c_im"][l, d, g].T
    s5b = np.zeros((DEPTH, 128, 4, 16, 16), f)
    for l in range(DEPTH):
        for d in range(2):
            for g in range(16):
                q, gl = g // 2, g % 2
                rows = slice(gl * 64, (gl + 1) * 64)
                s5b[l, rows, 0, d * 8 + q] = inp["s5_b_re"][l, d, g]
                s5b[l, rows, 1, d * 8 + q] = inp["s5_b_im"][l, d, g]
                s5b[l, rows, 2, d * 8 + q] = inp["s5_c_re"][l, d, g].T
                s5b[l, rows, 3, d * 8 + q] = inp["s5_c_im"][l, d, g].T
    sh["s5b"] = s5b.reshape(DEPTH, 128, 1024)
    cst2 = np.zeros((128, 8 * 240 + 256 + 32), f)
    for a in range(8):
        for h in range(16):
            cst2[a * 16 + h, a * 240 + 7 * 16 + h] = 1.0
    for k in range(8):
        for h in range(16):
            for t in range(8):
                if t >= k:
                    cst2[k * 16 + h, 1920 + t * 16: 1920 + (t + 1) * 16] = 1.0
                if k >= t:
                    cst2[k * 16 + h, 2048 + t * 16: 2048 + (t + 1) * 16] = 1.0
    cst2[:, 2176:2192] = np.arange(-7, 9, dtype=f)[None, :]
    cst2[:, 2192:2208] = np.arange(8, -8, -1, dtype=f)[None, :]
    sh["cst2"] = cst2
    sh["s5c"] = s5c.reshape(DEPTH, 128, 5 * 256)
    sh["s5s"] = s5s.reshape(DEPTH, 128, 48)
    sh["wc"] = wc.reshape(DEPTH, 128, 32 * 128)
    cg, sg = rope_tables(64)
    cm, sm = rope_tables(32)
    sh["ropeg"] = np.stack([np.tile(cg, (2, 1)), np.tile(sg, (2, 1))]).astype(f)
    sh["ropem"] = np.stack([np.tile(cm, (4, 1)), np.tile(sm, (4, 1))]).astype(f)
    cst = np.zeros((128, 128 + 128 + 16 + 512), f)
    cst[:, 0:128] = np.eye(128, dtype=f)
    for p in range(128):
        cst[p, 128 + (p // 64) * 64: 128 + (p // 64) * 64 + 64] = 1.0
    for p in range(128):
        g8 = p // 16
        for q in range(8):
            for gl in range(2):
                if g8 == (2 * q + gl) % 8:
                    cst[p, 256 + q * 2 + gl] = 1.0
    cst[:, 272:272 + 512] = np.arange(512, dtype=f)[None, :]
    sh["cst"] = cst
    xs = [np.ascontiguousarray(inp["x"][b].T).astype(f) for b in range(inp["x"].shape[0])]
    return sh, xs


V_ATTN, V_MLAQ, V_MLAKV, V_GQ, V_GQS, V_GK, V_GKS = 0, 8, 10, 11, 12, 13, 14
V_MIX, V_FFN, V_S5D, V_BGLU, V_CVB, V_LNG, V_LNB, V_FIN, V_CVW, V_FCW, V_FCB = 15, 23, 31, 33, 35, 37, 39, 41, 49, 111, 243
NV = 287


def build(stop_after=None, dbg=False):
    nc = bass.Bass("TRN2", target_bir_lowering=False)
    kb = KB(nc)

    def din(name, shape, dt=F32):
        return nc.dram_tensor(name, list(shape), dt, kind="ExternalInput").ap()

    def dscr(name, shape, dt):
        kind = "ExternalOutput" if dbg else "Internal"
        return nc.dram_tensor(name, list(shape), dt, kind=kind).ap()

    xT = din("xT", [D, L])
    w_in = din("w_in", [DEPTH, D, NCOL])
    w_uq = din("w_uq", [DEPTH, 256, 512])
    w_ukv = din("w_ukv", [DEPTH, 128, 512])
    w_glu = din("w_glu", [DEPTH, 256, 256])
    w_out = din("w_out", [DEPTH, D, D])
    w_up = din("w_up", [DEPTH, D, 2 * DFF])
    w_down = din("w_down", [DEPTH, DFF, D])
    vec_d = din("vec", [DEPTH, 128, NV])
    s5c_d = din("s5c", [DEPTH, 128, 1280])
    s5s_d = din("s5s", [DEPTH, 128, 48])
    wc_d = din("wc", [DEPTH, 128, 4096])
    ropeg_d = din("ropeg", [2, 128, L])
    ropem_d = din("ropem", [2, 128, L])
    cst_d = din("cst", [128, 784])
    s5b_d = din("s5b", [DEPTH, 128, 1024])
    cst2_d = din("cst2", [128, 2208])
    outT = nc.dram_tensor("outT", [D, L], F32, kind="ExternalOutput").ap()

    xres = dscr("xres", [D, L], F32)
    x1T = dscr("x1T", [D, L], F32)
    hn2T = dscr("hn2T", [D, L], BF16)
    uT = dscr("uT", [256, L], BF16)
    gT = dscr("gT", [256, L], BF16)
    qm = dscr("qm", [4, 96, L], BF16)
    km = dscr("km", [4, 96, L], BF16)
    vm = dscr("vm", [L, 256], BF16)
    qg = dscr("qg", [4, 64, L], BF16)
    kg = dscr("kg", [2, 64, L], BF16)
    vg = dscr("vg", [L, 128], BF16)
    mixT = dscr("mixT", [D, L], BF16)
    actT = dscr("actT", [DFF, L], BF16)

    gst = contextlib.ExitStack()
    cst = kb.sb(gst, [128, 784], F32, "cst")
    kb.dma("sp", [], [cst], cst[:], cst_d)
    ident = cst[:, 0:128]
    jrow = cst[:, 272:784]
    onesb = kb.sb(gst, [128, 128], BF16, "onesb")
    kb.memset([onesb], onesb[:], 1.0)
    bones = kb.sb(gst, [128, 128], BF16, "bones")
    kb.op("dve", [cst], [bones], lambda: nc.vector.tensor_copy(out=bones[:], in_=cst[:, 128:256]))
    onesf = kb.sb(gst, [128, 128], F32, "onesf")
    kb.memset([onesf], onesf[:], 1.0)
    vecs = []
    for l in range(DEPTH):
        v = kb.sb(gst, [128, NV], F32, "vec")
        kb.dma("sp", [], [v], v[:], vec_d[l])
        vecs.append(v)
    pbig = [kb.ps(gst, [128, 1024], F32, "ps") for _ in range(4)]
    psp = [TlV(pbig[i // 2].h, (i % 2) * 512, 512) for i in range(8)]
    pst = {"i": 0}

    def nps(n=6):
        t = psp[pst["i"] % n]
        pst["i"] += 1
        return t

    def rms_rstd(st_tiles, sq_srcs, nred, out_rstd, scr, bias, scale, ones=None):
        ss = psp[7]
        kb.mmg(ss, ss[:], [((ones or onesb)[:], ap) for (_, ap) in sq_srcs], [t for (t, _) in sq_srcs] + [ones or onesb])
        kb.act([ss, epsb], [scr], scr[:], ss[:], AF.Sqrt, scale=scale, bias=bias)
        kb.recip([scr], [out_rstd], out_rstd[:], scr[:])

    halfpi = kb.sb(gst, [128, 1], F32, "halfpi")
    kb.memset([halfpi], halfpi[:], math.pi / 2.0)
    kb.halfpi = halfpi
    epsb = kb.sb(gst, [128, 2], F32, "epsb")
    kb.memset([epsb], epsb[:, 0:1], EPS)
    kb.memset([epsb], epsb[:, 1:2], 64.0 * EPS)
    eps_ap = epsb[:, 0:1]
    eps64_ap = epsb[:, 1:2]

    def group_norm_store(st, zt, zaps, gv, gcol, row0, cols, sq, scr, rstd, ob):
        for j in range(2):
            kb.act([zt], [sq], sq[:, j, :], zaps[j], AF.Square)
        rms_rstd(None, [(sq, sq[:, 0, :]), (sq, sq[:, 1, :])], 2, rstd, scr, eps_ap, 1.0 / 256.0)
        for j in range(2):
            kb.stt([zt, gv, rstd], [ob], ob[:, j, :], zaps[j], gv[:, gcol + j:gcol + j + 1], rstd[:], ALU.mult, ALU.mult)
            kb.dma("sp", [ob], [], mixT[row0 + j * 128: row0 + (j + 1) * 128, cols], ob[:, j, :])

    for l in range(DEPTH):
        vv = vecs[l]
        xin = xT if l == 0 else xres
        with contextlib.ExitStack() as st:
            WIB = [0, 1024, NCOL]
            WiT = [kb.sb(st, [128, 8, WIB[i + 1] - WIB[i]], BF16, "Wi") for i in range(2)]
            for i in range(2):
                for kc in range(8):
                    kb.dma("pool", [], [WiT[i]], WiT[i][:, kc, :], w_in[l, kc * 128:(kc + 1) * 128, WIB[i]:WIB[i + 1]], part=(kc % 8 != 0))

            def wi(kc, c0, m):
                i = min(c0 // 1024, 1)
                return WiT[i], WiT[i][:, kc, c0 - WIB[i]:c0 - WIB[i] + m]
            Wuq = kb.sb(st, [128, 2, 512], BF16, "Wuq")
            for kc in range(2):
                kb.dma("pool", [], [Wuq], Wuq[:, kc, :], w_uq[l, kc * 128:(kc + 1) * 128, :])
            Wukv = kb.sb(st, [128, 512], BF16, "Wukv")
            kb.dma("pool", [], [Wukv], Wukv[:], w_ukv[l])
            rg = kb.sb(st, [128, 2, L], F32, "ropeg")
            rm = kb.sb(st, [128, 2, L], F32, "ropem")
            for i in range(2):
                kb.dma("sp", [], [rg], rg[:, i, :], ropeg_d[i])
                kb.dma("sp", [], [rm], rm[:, i, :], ropem_d[i])
            xb = [kb.sb(st, [128, 8, TB], F32, "xb") for _ in range(2)]
            sqb = kb.sb(st, [128, 8, TB], BF16, "sqb")
            hn = kb.sb(st, [128, 8, TB], BF16, "hn")
            scr = kb.sb(st, [128, TB], F32, "scr")
            rstd = kb.sb(st, [128, TB], F32, "rstd")
            stg = [kb.sb(st, [128, TB], BF16, "stg") for _ in range(5)]
            f1 = [kb.sb(st, [128, TB], F32, "f1") for _ in range(5)]
            ql = kb.sb(st, [128, 2, TB], F32, "ql")
            qn = kb.sb(st, [128, 2, TB], BF16, "qn")
            kvn = kb.sb(st, [128, TB], BF16, "kvn")
            sq2 = kb.sb(st, [128, 2, TB], BF16, "sq2")
            sqK = kb.sb(st, [128, TB], BF16, "sqK")
            sqG = kb.sb(st, [128, TB], BF16, "sqG")
            vst = [kb.sb(st, [128, 256], BF16, "vst") for _ in range(2)]
            si = {"s": 0, "f": 0, "v": 0}

            def nstg():
                si["s"] += 1
                return stg[si["s"] % 5]

            def nf1():
                si["f"] += 1
                return f1[si["f"] % 5]

            def load_x(tb):
                kb.dma("sp", [], [xb[tb % 2]], xb[tb % 2][:], xin.rearrange("(kc p) t -> p kc t", p=128)[:, :, tb * TB:(tb + 1) * TB])

            load_x(0)
            for tb in range(NB):
                cs = slice(tb * TB, (tb + 1) * TB)
                x_ = xb[tb % 2]
                if tb + 1 < NB:
                    load_x(tb + 1)
                kb.act([x_], [sqb], sqb[:], x_[:], AF.Square)
                rms_rstd(None, [(sqb, sqb[:, kc, :]) for kc in range(8)], 8, rstd, scr, eps_ap, 1.0 / D)
                for kc in range(8):
                    kb.stt([x_, vv, rstd], [hn], hn[:, kc, :], x_[:, kc, :], vv[:, V_ATTN + kc:V_ATTN + kc + 1], rstd[:],
                           ALU.mult, ALU.mult)

                def proj(c0, m=128):
                    p = nps()
                    kb.mmg(p, p[0:m, :], [(wi(kc, c0, m)[1], hn[:, kc, :]) for kc in range(8)], [wi(0, c0, m)[0], hn])
                    return p

                for j in range(2):
                    p = proj(768 + j * 128)
                    kb.act([p], [ql], ql[:, j, :], p[:], AF.Copy)
                    kb.act([p], [sq2], sq2[:, j, :], p[:], AF.Square)
                p = proj(1024)
                fk = nf1()
                kb.act([p], [fk], fk[:], p[:], AF.Copy)
                kb.act([p], [sqK], sqK[:], p[:], AF.Square)
                for j in range(2):
                    p = proj(j * 128)
                    s_ = nstg()
                    kb.act([p], [s_], s_[:], p[:], AF.Copy)
                    kb.dma("sp", [s_], [], uT[j * 128:(j + 1) * 128, cs], s_[:])
                for j in range(2):
                    pv = proj(256 + j * 128)
                    pg = proj(512 + j * 128)
                    f_ = nf1()
                    kb.act([pg], [f_], f_[:], pg[:], AF.Sigmoid)
                    s_ = nstg()
                    kb.tt([pv, f_], [s_], s_[:], pv[:], f_[:], ALU.mult)
                    kb.dma("sp", [s_], [], gT[j * 128:(j + 1) * 128, cs], s_[:])
                rq = nf1()
                rms_rstd(None, [(sq2, sq2[:, 0, :]), (sq2, sq2[:, 1, :])], 2, rq, scr, eps_ap, 1.0 / 256.0)
                for j in range(2):
                    kb.stt([ql, vv, rq], [qn], qn[:, j, :], ql[:, j, :], vv[:, V_MLAQ + j:V_MLAQ + j + 1], rq[:], ALU.mult, ALU.mult)
                rk = nf1()
                rms_rstd(None, [(sqK, sqK[:])], 1, rk, scr, eps_ap, 1.0 / 128.0)
                kb.stt([fk, vv, rk], [kvn], kvn[:], fk[:], vv[:, V_MLAKV:V_MLAKV + 1], rk[:], ALU.mult, ALU.mult)
                def gqa_part(c_main, c_swap, vg_col, vgs_col, bias_ap, scale, dst, heads):
                    p = proj(c_main)
                    p2 = proj(c_swap)
                    qa = nf1()
                    kb.act([p, vv], [qa], qa[:], p[:], AF.Identity, scale=vv[:, vg_col:vg_col + 1])
                    kb.act([p], [sqG], sqG[:], p[:], AF.Square)
                    qs = nf1()
                    kb.act([p2, vv], [qs], qs[:], p2[:], AF.Identity, scale=vv[:, vgs_col:vgs_col + 1])
                    rr = nf1()
                    rms_rstd(None, [(sqG, sqG[:])], 1, rr, scr, bias_ap, scale, ones=bones)
                    kb.tt([qa, rg], [qa], qa[:], qa[:], rg[:, 0, cs], ALU.mult)
                    kb.tt([qs, rg], [qs], qs[:], qs[:], rg[:, 1, cs], ALU.mult)
                    kb.tt([qa, qs], [qa], qa[:], qa[:], qs[:], ALU.add)
                    s_ = nstg()
                    kb.tt([qa, rr], [s_], s_[:], qa[:], rr[:], ALU.mult)
                    for hh in range(2):
                        kb.dma("sp", [s_], [], dst[heads[hh], :, cs], s_[hh * 64:(hh + 1) * 64, :])

                for j in range(2):
                    gqa_part(1152 + j * 128, 1408 + j * 128, V_GQ, V_GQS, eps64_ap, 1.0, qg, (2 * j, 2 * j + 1))
                gqa_part(1664, 1792, V_GK, V_GKS, eps_ap, 1.0 / 64.0, kg, (0, 1))
                sc_m = 96.0 ** -0.5
                for j in range(2):
                    p = nps()
                    kb.mmg(p, p[:], [(Wuq[:, kc, j * 128:(j + 1) * 128], qn[:, kc, :]) for kc in range(2)], [Wuq, qn])
                    s_ = nstg()
                    kb.act([p], [s_], s_[:], p[:], AF.Copy, scale=sc_m)
                    for hh in range(2):
                        kb.dma("sp", [s_], [], qm[2 * j + hh, 0:64, cs], s_[hh * 64:(hh + 1) * 64, :])
                pc = nps()
                kb.mmg(pc, pc[:], [(Wuq[:, kc, 256:384], qn[:, kc, :]) for kc in range(2)], [Wuq, qn])
                pd = nps()
                kb.mmg(pd, pd[:], [(Wuq[:, kc, 384:512], qn[:, kc, :]) for kc in range(2)], [Wuq, qn])
                fa = nf1()
                kb.tt([pc, rm], [fa], fa[:], pc[:], rm[:, 0, cs], ALU.mult)
                fb = nf1()
                kb.tt([pd, rm], [fb], fb[:], pd[:], rm[:, 1, cs], ALU.mult)
                kb.tt([fa, fb], [fa], fa[:], fa[:], fb[:], ALU.add)
                s_ = nstg()
                kb.act([fa], [s_], s_[:], fa[:], AF.Copy, scale=sc_m)
                for h in range(4):
                    kb.dma("sp", [s_], [], qm[h, 64:96, cs], s_[h * 32:(h + 1) * 32, :])
                for j in range(2):
                    p = nps()
                    kb.mmg(p, p[:], [(Wukv[:, j * 128:(j + 1) * 128], kvn[:])], [Wukv, kvn])
                    s_ = nstg()
                    kb.act([p], [s_], s_[:], p[:], AF.Copy)
                    for hh in range(2):
                        kb.dma("sp", [s_], [], km[2 * j + hh, 0:64, cs], s_[hh * 64:(hh + 1) * 64, :])
                for sbk in range(4):
                    p = nps()
                    kb.mmg(p, p[:, 0:256], [(kvn[:, sbk * 128:(sbk + 1) * 128], Wukv[:, 256:512])], [Wukv, kvn])
                    v_ = vst[sbk % 2]
                    kb.act([p], [v_], v_[:], p[:, 0:256], AF.Copy)
                    kb.dma("sp", [v_], [], vm[tb * TB + sbk * 128: tb * TB + (sbk + 1) * 128, :], v_[:])
                pa = proj(2048, 32)
                pb = proj(2080, 32)
                fa = nf1()
                kb.tt([pa, rm], [fa], fa[0:32, :], pa[0:32, :], rm[0:32, 0, cs], ALU.mult)
                fb = nf1()
                kb.tt([pb, rm], [fb], fb[0:32, :], pb[0:32, :], rm[0:32, 1, cs], ALU.mult)
                s_ = nstg()
                kb.tt([fa, fb], [s_], s_[0:32, :], fa[0:32, :], fb[0:32, :], ALU.add)
                for h in range(4):
                    kb.dma("sp", [s_], [], km[h, 64:96, cs], s_[0:32, :])
                for sbk in range(4):
                    p = nps()
                    kb.mmg(p, p[:, 0:128], [(hn[:, kc, sbk * 128:(sbk + 1) * 128], wi(kc, 1920, 128)[1]) for kc in range(8)], [WiT[1], hn])
                    v_ = vst[sbk % 2]
                    kb.act([p], [v_], v_[:, 0:128], p[:, 0:128], AF.Copy)
                    kb.dma("sp", [v_], [], vg[tb * TB + sbk * 128: tb * TB + (sbk + 1) * 128, :], v_[:, 0:128])
            kb.phase_end()
        if stop_after == ("P1", l):
            break
        with contextlib.ExitStack() as st:
            emit_s5h(kb, nc, st, l, vv, cst, jrow, s5s_d, s5b_d, cst2_d, w_glu, uT, psp, nps, group_norm_store, onesb)
            kb.phase_end()
        if stop_after == ("P2", l):
            break
        with contextlib.ExitStack() as st:
            dg = kb.sb(st, [128, 62, 128], BF16, "dg")
            for i in range(62):
                kb.ts([cst, vv], [dg], dg[:, i, :], ident, vv[:, V_CVW + i:V_CVW + i + 1], ALU.mult)
            gb = [kb.sb(st, [128, 2, TB + 30], BF16, "gb") for _ in range(2)]
            def load_g(tb):
                g_ = gb[tb % 2]
                lo = tb * TB - 15
                hi = tb * TB + TB + 15
                a, b = max(lo, 0), min(hi, L)
                if lo < 0 or hi > L:
                    kb.memset([g_], g_[:], 0.0)
                kb.dma("sp", [], [g_], g_[:, :, a - lo: b - lo], gT.rearrange("(m p) t -> p m t", p=128)[:, :, a:b])

            sets = []
            for _ in range(2):
                d_ = {}
                for nm in ("cv", "cq", "xn", "sgm"):
                    d_[nm] = kb.sb(st, [128, 2, TB], F32, nm)
                for nm in ("mean", "var", "scr", "rstd"):
                    d_[nm] = kb.sb(st, [128, TB], F32, nm)
                d_["sq"] = kb.sb(st, [128, 2, TB], BF16, "sq")
                d_["ob"] = kb.sb(st, [128, 2, TB], BF16, "ob")
                sets.append(d_)
            for tb in range(NB):
                d_ = sets[tb % 2]
                cv, cq, xn, sgm = d_["cv"], d_["cq"], d_["xn"], d_["sgm"]
                mean, var, scr, rstd, sq, ob = d_["mean"], d_["var"], d_["scr"], d_["rstd"], d_["sq"], d_["ob"]
                g_ = gb[tb % 2]
                if tb == 0:
                    load_g(0)
                if tb + 1 < NB:
                    load_g(tb + 1)
                for m in range(2):
                    p = nps(5)
                    kb.mmg(p, p[:], [(dg[:, m * 31 + k, :], g_[:, m, k:k + TB]) for k in range(31)], [dg, g_])
                    kb.act([p, vv], [cv], cv[:, m, :], p[:], AF.Identity, bias=vv[:, V_CVB + m:V_CVB + m + 1])
                    kb.act([cv], [cq], cq[:, m, :], cv[:, m, :], AF.Square)
                s1 = psp[6] if tb % 2 == 0 else psp[5]
                kb.mmg(s1, s1[:], [(onesf[:], cv[:, m, :]) for m in range(2)], [onesf, cv])
                s2 = psp[7]
                kb.mmg(s2, s2[:], [(onesf[:], cq[:, m, :]) for m in range(2)], [onesf, cq])
                kb.act([s1], [mean], mean[:], s1[:], AF.Copy, scale=1.0 / 256.0)
                kb.tt([mean], [var], var[:], mean[:], mean[:], ALU.mult)
                kb.stt([s2, var], [var], var[:], s2[:], 1.0 / 256.0, var[:], ALU.mult, ALU.subtract)
                kb.act([var, epsb], [scr], scr[:], var[:], AF.Sqrt, bias=eps_ap)
                kb.recip([scr], [rstd], rstd[:], scr[:])
                for m in range(2):
                    kb.tt([cv, mean], [cv], cv[:, m, :], cv[:, m, :], mean[:], ALU.subtract)
                    kb.tt([cv, rstd], [cv], cv[:, m, :], cv[:, m, :], rstd[:], ALU.mult)
                    kb.act([cv, vv], [xn], xn[:, m, :], cv[:, m, :], AF.Identity, scale=vv[:, V_LNG + m:V_LNG + m + 1],
                           bias=vv[:, V_LNB + m:V_LNB + m + 1])
                    kb.act([cv, vv], [sgm], sgm[:, m, :], cv[:, m, :], AF.Sigmoid, scale=vv[:, V_LNG + m:V_LNG + m + 1],
                           bias=vv[:, V_LNB + m:V_LNB + m + 1])
                    kb.tt([xn, sgm], [xn], xn[:, m, :], xn[:, m, :], sgm[:, m, :], ALU.mult)
                group_norm_store(st, xn, [xn[:, 0, :], xn[:, 1, :]], vv, V_MIX + 2, 256, slice(tb * TB, (tb + 1) * TB), sq, scr, rstd, ob)
            kb.phase_end()
        if stop_after == ("P3", l):
            break
        with contextlib.ExitStack() as st:
            qt = [[kb.sb(st, [128, L], BF16, "qt") for _ in range(2)] for _ in range(2)]
            kt = [[kb.sb(st, [128, L], BF16, "kt") for _ in range(2)] for _ in range(2)]
            va = [[kb.sb(st, [128, 32, 128], BF16, "va") for _ in range(2)] for _ in range(2)]
            for s_ in range(2):
                for i in range(2):
                    kb.memset([va[s_][i]], va[s_][i][:], 1.0, e="pool" if i else "dve")
            yg = [kb.sb(st, [128, L], F32, "yg") for _ in range(2)]
            pt = [kb.sb(st, [128, 2 * TB], BF16, "pt") for _ in range(3)]
            rec = [kb.sb(st, [128, TB], F32, "rec") for _ in range(2)]
            scr = kb.sb(st, [128, TB], F32, "scr")
            rstd = kb.sb(st, [128, TB], F32, "rstd")
            sq = kb.sb(st, [128, 2, TB], BF16, "sq")
            ob = kb.sb(st, [128, 2, TB], BF16, "ob")
            mx = [kb.sb(st, [128, 2, 2, 10], F32, "mx") for _ in range(2)]
            negb = [kb.sb(st, [128, 2], F32, "negb") for _ in range(2)]
            S2 = [Tl(pbig[0].h), Tl(pbig[1].h)]
            Oh = [psp[4], psp[5]]
            pairs = [(0, 0), (0, 1), (1, 0), (1, 1)]

            sqq4 = [kb.sb(st, [96, L], BF16, "sqq4") for _ in range(4)]

            def setup_load(pidx):
                grp, pr = pairs[pidx]
                s_ = pidx % 2
                for i in range(2):
                    h = 2 * pr + i
                    q_, k_, v_ = qt[s_][i], kt[s_][i], va[s_][i]
                    if grp == 1:
                        kb.memset([q_], q_[64:128, :], 0.0, e="pool")
                        kb.memset([k_], k_[64:128, :], 0.0, e="pool")
                        kb.dma("sp", [], [q_], q_[0:64, :], qg[h])
                        kb.dma("sp", [], [k_], k_[0:64, :], kg[pr])
                        kb.dma("sp", [], [v_], v_[:, :, i * 64:(i + 1) * 64],
                               vg.rearrange("(kb p) c -> p kb c", p=128)[:, :, pr * 64:(pr + 1) * 64])
                    else:
                        kb.dma("sp", [], [q_], q_[0:96, :], qm[h])
                        kb.dma("sp", [], [k_], k_[0:96, :], km[h])
                        kb.dma("sp", [], [v_], v_[:, :, i * 64:(i + 1) * 64],
                               vm.rearrange("(kb p) c -> p kb c", p=128)[:, :, h * 64:(h + 1) * 64])

            def setup_sq(pidx, only=None):
                grp, pr = pairs[pidx]
                s_ = pidx % 2
                dk = 96 if grp == 0 else 64
                for i in range(2):
                    for w_, src in ((0, qt[s_][i]), (1, kt[s_][i])):
                        if only is not None and only != 2 * i + w_:
                            continue
                        kb.act([src], [sqq4[2 * i + w_]], sqq4[2 * i + w_][0:dk, :], src[0:dk, :], AF.Square)

            def setup_bound(pidx):
                grp, pr = pairs[pidx]
                s_ = pidx % 2
                dk = 96 if grp == 0 else 64
                mx_ = mx[s_]
                for i in range(2):
                    for w_ in range(2):
                        sq_ = sqq4[2 * i + w_]
                        for blk in range(NB):
                            pss_ = psp[6 + (blk % 2)]
                            kb.mmg(pss_, pss_[:], [(onesb[0:dk, :], sq_[0:dk, blk * TB:(blk + 1) * TB])], [onesb, sq_])
                            kb.op("dve", [pss_], [mx_], lambda: nc.vector.reduce_max(
                                out=mx_[:, i, w_, blk:blk + 1], in_=pss_[:], axis=mybir.AxisListType.X))
                        kb.op("dve", [mx_], [mx_], lambda: nc.vector.reduce_max(
                            out=mx_[:, i, w_, 8:9], in_=mx_[:, i, w_, 0:8], axis=mybir.AxisListType.X))
                    kb.tt([mx_], [mx_], mx_[:, i, 0, 9:10], mx_[:, i, 0, 8:9], mx_[:, i, 1, 8:9], ALU.mult)
                    kb.act([mx_], [mx_], mx_[:, i, 1, 9:10], mx_[:, i, 0, 9:10], AF.Sqrt)
                    kb.ts([mx_], [negb[s_]], negb[s_][:, i:i + 1], mx_[:, i, 1, 9:10], -1.0, ALU.mult)

            def setup(pidx):
                setup_load(pidx)
                setup_sq(pidx)
                setup_bound(pidx)

            def gnorm_block(grp, tb):
                cs = slice(tb * TB, (tb + 1) * TB)
                for j in range(2):
                    kb.act([yg[j]], [sq], sq[:, j, :], yg[j][:, cs], AF.Square)
                rms_rstd(None, [(sq, sq[:, 0, :]), (sq, sq[:, 1, :])], 2, rstd, scr, eps_ap, 1.0 / 256.0)
                gc = V_MIX + 4 + 2 * grp
                for j in range(2):
                    kb.stt([yg[j], vv, rstd], [ob], ob[:, j, :], yg[j][:, cs], vv[:, gc + j:gc + j + 1], rstd[:], ALU.mult, ALU.mult)
                    r0 = 512 + 256 * grp + j * 128
                    kb.dma("sp", [ob], [], mixT[r0:r0 + 128, cs], ob[:, j, :])

            setup(0)
            deferred = []
            pi = 0
            for pidx in range(4):
                grp, pr = pairs[pidx]
                s_ = pidx % 2
                dkm = 96 if grp == 0 else 128
                pending = []
                for qb in range(NB):
                    qs_ = slice(qb * TB, (qb + 1) * TB)
                    if deferred:
                        gnorm_block(*deferred.pop(0))
                    if pidx + 1 < 4:
                        if qb == 1:
                            setup_load(pidx + 1)
                        elif 2 <= qb <= 5:
                            setup_sq(pidx + 1, only=qb - 2)
                        elif qb == 6:
                            setup_bound(pidx + 1)
                    for i in range(2):
                        O = Oh[i]
                        q_, k_, v_ = qt[s_][i], kt[s_][i], va[s_][i]
                        for kb2 in range(16):
                            S = S2[pi % 2]
                            P = pt[pi % 3]
                            pi += 1
                            for hf in range(2):
                                kbk = 2 * kb2 + hf
                                kb.mm_acc(S, S[:, hf * TB:(hf + 1) * TB], k_[0:dkm, kbk * 128:(kbk + 1) * 128], q_[0:dkm, qs_],
                                          [k_, q_], True, True, hf == 0)
                            kb.act([S, negb[s_]], [P], P[:], S[:], AF.Exp, bias=negb[s_][:, i:i + 1])

                            def fin(O=O, P=P, i=i, kb2=kb2, qs_=qs_, pr=pr, v_=v_):
                                for hf in range(2):
                                    kbk = 2 * kb2 + hf
                                    first = (kb2 == 0 and hf == 0)
                                    last = (kb2 == 15 and hf == 1)
                                    kb.mm_acc(O, O[:], v_[:, kbk, :], P[:, hf * TB:(hf + 1) * TB], [v_, P], first, last, first)
                                if kb2 == 15:
                                    a0, d0 = (0, 64) if i == 0 else (64, 0)
                                    kb.recip([O], [rec[i]], rec[i][a0:a0 + 64, :], O[d0:d0 + 64, :])
                                    kb.tt([O, rec[i]], [yg[pr]], yg[pr][a0:a0 + 64, qs_], O[a0:a0 + 64, :], rec[i][a0:a0 + 64, :], ALU.mult)

                            pending.append(fin)
                            if len(pending) > 1:
                                pending.pop(0)()
                while pending:
                    pending.pop(0)()
                if pr == 1:
                    deferred += [(grp, tb) for tb in range(NB)]
            while deferred:
                gnorm_block(*deferred.pop(0))
            kb.phase_end()
        if stop_after == ("P4", l):
            break
        with contextlib.ExitStack() as st:
            Wo = [kb.sb(st, [128, 8, 512], BF16, "Wo") for _ in range(2)]
            for cg in range(2):
                for kc in range(8):
                    kb.dma("pool", [], [Wo[cg]], Wo[cg][:, kc, :], w_out[l, kc * 128:(kc + 1) * 128, cg * 512:(cg + 1) * 512], part=(kc % 8 != 0))
            mb = [kb.sb(st, [128, 8, TB], BF16, "mb") for _ in range(2)]
            xb = [kb.sb(st, [128, 8, TB], F32, "xb") for _ in range(2)]
            x1 = [kb.sb(st, [128, 8, TB], F32, "x1") for _ in range(2)]
            sqb = kb.sb(st, [128, 8, TB], BF16, "sqb")
            hb = [kb.sb(st, [128, 8, TB], BF16, "hb") for _ in range(2)]
            scr = kb.sb(st, [128, TB], F32, "scr")
            rstd = kb.sb(st, [128, TB], F32, "rstd")
            def load_a(tb):
                cs = slice(tb * TB, (tb + 1) * TB)
                kb.dma("sp", [], [mb[tb % 2]], mb[tb % 2][:], mixT.rearrange("(kc p) t -> p kc t", p=128)[:, :, cs])
                kb.dma("sp", [], [xb[tb % 2]], xb[tb % 2][:], xin.rearrange("(kc p) t -> p kc t", p=128)[:, :, cs])

            load_a(0)
            for tb in range(NB):
                cs = slice(tb * TB, (tb + 1) * TB)
                m_, x_, x1_, h_ = mb[tb % 2], xb[tb % 2], x1[tb % 2], hb[tb % 2]
                if tb + 1 < NB:
                    load_a(tb + 1)
                for oc in range(8):
                    p = nps()
                    kb.mmg(p, p[:], [(Wo[oc // 4][:, kc, (oc % 4) * 128:(oc % 4 + 1) * 128], m_[:, kc, :]) for kc in range(8)], [Wo[oc // 4], m_])
                    kb.tt([p, x_], [x1_], x1_[:, oc, :], p[:], x_[:, oc, :], ALU.add)
                kb.dma("act", [x1_], [], x1T.rearrange("(kc p) t -> p kc t", p=128)[:, :, cs], x1_[:])
                kb.act([x1_], [sqb], sqb[:], x1_[:], AF.Square)
                rms_rstd(None, [(sqb, sqb[:, kc, :]) for kc in range(8)], 8, rstd, scr, eps_ap, 1.0 / D)
                for kc in range(8):
                    kb.stt([x1_, vv, rstd], [h_], h_[:, kc, :], x1_[:, kc, :], vv[:, V_FFN + kc:V_FFN + kc + 1], rstd[:], ALU.mult, ALU.mult)
                kb.dma("act", [h_], [], hn2T.rearrange("(kc p) t -> p kc t", p=128)[:, :, cs], h_[:])
            kb.phase_end()
        wd_scope = contextlib.ExitStack()
        Wd = [kb.sb(wd_scope, [128, 22, 512], BF16, "Wd") for _ in range(2)]
        with contextlib.ExitStack() as st:
            WuG = []
            for g in range(3):
                w = min(8, 22 - 8 * g) * 128
                t_ = kb.sb(st, [128, 8, 2 * w], BF16, "WuG")
                for kc in range(8):
                    kb.dma("pool", [], [t_], t_[:, kc, 0:w], w_up[l, kc * 128:(kc + 1) * 128, g * 1024:g * 1024 + w], part=(kc % 8 != 0))
                    kb.dma("pool", [], [t_], t_[:, kc, w:2 * w], w_up[l, kc * 128:(kc + 1) * 128, DFF + g * 1024:DFF + g * 1024 + w], part=True)
                WuG.append((t_, w))
            for cg in range(2):
                for kc in range(22):
                    kb.dma("pool", [], [Wd[cg]], Wd[cg][:, kc, :], w_down[l, kc * 128:(kc + 1) * 128, cg * 512:(cg + 1) * 512], part=(kc % 8 != 0))
            hbk = [kb.sb(st, [128, 8, TB], BF16, "hbk") for _ in range(2)]
            ag = [kb.sb(st, [128, TB], F32, "ag") for _ in range(2)]
            au = [kb.sb(st, [128, TB], F32, "au") for _ in range(2)]
            sg_ = [kb.sb(st, [128, TB], F32, "sg") for _ in range(2)]
            ab = [kb.sb(st, [128, TB], BF16, "ab") for _ in range(3)]
            SV = [kb.sb(st, [128, 44, 2], F32, "SV") for _ in range(2)]
            kb.memset([SV[0]], SV[0][:], 0.0)
            it = 0
            for tb in range(NB):
                h_ = hbk[tb % 2]
                kb.dma("sp", [], [h_], h_[:], hn2T.rearrange("(kc p) t -> p kc t", p=128)[:, :, tb * TB:(tb + 1) * TB])
                svp, svn = SV[tb % 2], SV[(tb + 1) % 2]
                for j in range(22):
                    wt, w = WuG[j // 8]
                    accs = []
                    for half in range(2):
                        jj = half * 22 + j
                        c0 = half * w + (j % 8) * 128
                        p = nps(6)
                        kb.mmg(p, p[:], [(wt[:, kc, c0:c0 + 128], h_[:, kc, :]) for kc in range(8)], [wt, h_])
                        acc = (ag if half == 0 else au)[(it // 2) % 2]
                        wb = V_FCW + jj * 3
                        kb.act([p, vv], [acc], acc[:], p[:], AF.Identity, scale=vv[:, wb + 2:wb + 3], bias=vv[:, V_FCB + jj:V_FCB + jj + 1])
                        kb.stt([p, vv, acc], [acc], acc[:, 1:TB], p[:, 0:TB - 1], vv[:, wb + 1:wb + 2], acc[:, 1:TB], ALU.mult, ALU.add)
                        kb.stt([p, vv, acc], [acc], acc[:, 2:TB], p[:, 0:TB - 2], vv[:, wb:wb + 1], acc[:, 2:TB], ALU.mult, ALU.add)
                        kb.stt([svp, vv, acc], [acc], acc[:, 0:2], svp[:, jj, 0:2], vv[:, wb:wb + 1], acc[:, 0:2], ALU.mult, ALU.add)
                        kb.act([svp, vv, acc], [acc], acc[:, 0:1], svp[:, jj, 1:2], AF.Identity, scale=vv[:, wb + 1:wb + 2], bias=acc[:, 0:1])
                        kb.act([p], [svn], svn[:, jj, :], p[:, TB - 2:TB], AF.Copy)
                        accs.append(acc)
                        it += 1
                    g_, u_ = accs
                    s_ = sg_[j % 2]
                    kb.act([g_], [s_], s_[:], g_[:], AF.Silu)
                    a_ = ab[j % 3]
                    kb.tt([s_, u_], [a_], a_[:], s_[:], u_[:], ALU.mult, e="pool")
                    if tb == 0:
                        kb.dma("sp", [a_], [], actT[j * 128:(j + 1) * 128, 0:TB - 1], a_[:, 1:TB])
                    else:
                        kb.dma("sp", [a_], [], actT[j * 128:(j + 1) * 128, tb * TB - 1:tb * TB + TB - 1], a_[:])
            svl = SV[NB % 2]
            tl = kb.sb(st, [128, 44], F32, "tl")
            tl2 = kb.sb(st, [128, 44], F32, "tl2")
            tlb = kb.sb(st, [128, 22], BF16, "tlb")
            kb.tt([svl, vv], [tl], tl[:], svl[:, :, 1], vv[:, V_FCW + 1:V_FCW + 132:3], ALU.mult)
            kb.tt([svl, vv], [tl2], tl2[:], svl[:, :, 0], vv[:, V_FCW:V_FCW + 132:3], ALU.mult)
            kb.tt([tl, tl2], [tl], tl[:], tl[:], tl2[:], ALU.add)
            kb.tt([tl, vv], [tl], tl[:], tl[:], vv[:, V_FCB:V_FCB + 44], ALU.add)
            kb.act([tl], [tl2], tl2[:, 0:22], tl[:, 0:22], AF.Silu)
            kb.tt([tl2, tl], [tlb], tlb[:], tl2[:, 0:22], tl[:, 22:44], ALU.mult)
            with nc.allow_non_contiguous_dma(reason="single-column tail store"):
                kb.dma("sp", [tlb], [], actT.rearrange("(j p) t -> p j t", p=128)[:, :, L - 1:L], tlb[:].unsqueeze(2))
            kb.phase_end()
        with contextlib.ExitStack() as st:

            abk = [kb.sb(st, [128, 22, TB], BF16, "abk") for _ in range(2)]
            x1 = [kb.sb(st, [128, 8, TB], F32, "x1") for _ in range(2)]
            x2 = [kb.sb(st, [128, 8, TB], F32, "x2") for _ in range(1)]
            sqb = kb.sb(st, [128, 8, TB], BF16, "sqb")
            scr = kb.sb(st, [128, TB], F32, "scr")
            rstd = kb.sb(st, [128, TB], F32, "rstd")
            last = (l == DEPTH - 1)
            def load_c(tb):
                cs = slice(tb * TB, (tb + 1) * TB)
                kb.dma("sp", [], [abk[tb % 2]], abk[tb % 2][:], actT.rearrange("(kc p) t -> p kc t", p=128)[:, :, cs])
                kb.dma("sp", [], [x1[tb % 2]], x1[tb % 2][:], x1T.rearrange("(kc p) t -> p kc t", p=128)[:, :, cs])

            load_c(0)
            for tb in range(NB):
                cs = slice(tb * TB, (tb + 1) * TB)
                a_, x1_, x2_ = abk[tb % 2], x1[tb % 2], x2[0]
                if tb + 1 < NB:
                    load_c(tb + 1)
                for oc in range(8):
                    p = nps()
                    kb.mmg(p, p[:], [(Wd[oc // 4][:, kc, (oc % 4) * 128:(oc % 4 + 1) * 128], a_[:, kc, :]) for kc in range(22)], [Wd[oc // 4], a_])
                    kb.tt([p, x1_], [x2_], x2_[:, oc, :], p[:], x1_[:, oc, :], ALU.add)
                if not last:
                    kb.dma("act", [x2_], [], xres.rearrange("(kc p) t -> p kc t", p=128)[:, :, cs], x2_[:])
                else:
                    kb.act([x2_], [sqb], sqb[:], x2_[:], AF.Square)
                    rms_rstd(None, [(sqb, sqb[:, kc, :]) for kc in range(8)], 8, rstd, scr, eps_ap, 1.0 / D)
                    for kc in range(8):
                        kb.stt([x2_, vv, rstd], [x1_], x1_[:, kc, :], x2_[:, kc, :], vv[:, V_FIN + kc:V_FIN + kc + 1], rstd[:], ALU.mult, ALU.mult)
                    kb.dma("act", [x1_], [], outT.rearrange("(kc p) t -> p kc t", p=128)[:, :, cs], x1_[:])
            kb.phase_end()
        wd_scope.close()
        if stop_after == ("P5", l):
            break
    kb.phase_end()
    return nc


def sincos(kb, nc, arg_t, arg_ap, shape, out_s, out_s_ap, out_c, out_c_ap, k_t, r_t):
    C1 = 6.28125
    C2 = TWO_PI - 6.28125
    hp = kb.halfpi
    kb.ts([arg_t], [k_t], k_t[:], arg_ap, 1.0 / TWO_PI, ALU.mult, MAGIC, ALU.add)
    kb.ts([k_t], [k_t], k_t[:], k_t[:], -MAGIC, ALU.add)
    kb.stt([k_t, arg_t], [r_t], r_t[:], k_t[:], -C1, arg_ap, ALU.mult, ALU.add)
    kb.stt([k_t, r_t], [r_t], r_t[:], k_t[:], -C2, r_t[:], ALU.mult, ALU.add)
    kb.ts([r_t], [r_t], r_t[:], r_t[:], 3.14159, ALU.min, -3.14159, ALU.max)
    kb.act([r_t], [out_s], out_s_ap, r_t[:], AF.Sin)
    kb.act([r_t], [k_t], k_t[:], r_t[:], AF.Sin, scale=0.5)
    kb.tt([k_t], [k_t], k_t[:], k_t[:], k_t[:], ALU.mult)
    kb.ts([k_t], [out_c], out_c_ap, k_t[:], -2.0, ALU.mult, 1.0, ALU.add)


def emit_s5(kb, nc, st, l, vv, cst, jrow, s5c_d, s5s_d, wc_d, w_glu, uT, psp, nps, group_norm_store, eps_ap, rms_rstd):
    st_outer = st
    Wg = kb.sb(st_outer, [128, 2, 256], BF16, "Wg")
    for kc in range(2):
        kb.dma("pool", [], [Wg], Wg[:, kc, :], w_glu[l, kc * 128:(kc + 1) * 128, :])
    ub = kb.sb(st_outer, [128, 2, L], BF16, "ub")
    kb.dma("sp", [], [ub], ub[:], uT.rearrange("(m p) t -> p m t", p=128))
    Y = kb.sb(st_outer, [128, 2, L], F32, "Y")
    st = contextlib.ExitStack()
    pc = kb.sb(st, [128, 5, 256], F32, "s5c")
    kb.dma("sp", [], [pc], pc[:], s5c_d[l].rearrange("p (a b) -> p a b", a=5))
    pss = kb.sb(st, [128, 3, 16], F32, "s5s")
    kb.dma("sp", [], [pss], pss[:], s5s_d[l].rearrange("p (a b) -> p a b", a=3))
    wcf = kb.sb(st, [128, 2, 16, 128], BF16, "wc")
    for ri in range(2):
        for hf in range(2):
            kb.dma("pool", [], [wcf], wcf[:, ri, hf * 8:(hf + 1) * 8, :],
                   wc_d[l].rearrange("p (a b c) -> p a b c", a=2, b=16)[:, ri, hf * 8:(hf + 1) * 8, :])
    kb.ts([wcf], [wcf], wcf[:, 1], wcf[:, 1], -1.0, ALU.mult)
    def tmp(n, nm):
        return kb.sb(st, [128, n], F32, nm)

    n = 256
    dl, are, aim, mag, sn, cs_, k_t, r_t = [tmp(n, x) for x in ("dl", "are", "aim", "mag", "sn", "cs", "kt", "rt")]
    kb.act([pc], [dl], dl[:], pc[:, 2, :], AF.Exp)
    kb.tt([pc, dl], [are], are[:], pc[:, 0, :], dl[:], ALU.mult)
    kb.tt([pc, dl], [aim], aim[:], pc[:, 1, :], dl[:], ALU.mult)
    kb.act([are], [mag], mag[:], are[:], AF.Exp)
    sincos(kb, nc, aim, aim[:], n, sn, sn[:], cs_, cs_[:], k_t, r_t)
    ar1, ai, den, fr, fi, t1, t2 = [tmp(n, x) for x in ("ar1", "ai", "den", "fr", "fi", "t1", "t2")]
    kb.tt([mag, cs_], [ar1], ar1[:], mag[:], cs_[:], ALU.mult)
    kb.ts([ar1], [ar1], ar1[:], ar1[:], -1.0, ALU.add)
    kb.tt([mag, sn], [ai], ai[:], mag[:], sn[:], ALU.mult)
    kb.tt([pc], [den], den[:], pc[:, 0, :], pc[:, 0, :], ALU.mult)
    kb.tt([pc], [t1], t1[:], pc[:, 1, :], pc[:, 1, :], ALU.mult)
    kb.tt([den, t1], [den], den[:], den[:], t1[:], ALU.add)
    kb.recip([den], [den], den[:], den[:])
    kb.tt([ar1, pc], [fr], fr[:], ar1[:], pc[:, 0, :], ALU.mult)
    kb.tt([ai, pc], [t1], t1[:], ai[:], pc[:, 1, :], ALU.mult)
    kb.tt([fr, t1], [fr], fr[:], fr[:], t1[:], ALU.add)
    kb.tt([fr, den], [fr], fr[:], fr[:], den[:], ALU.mult)
    kb.tt([ai, pc], [fi], fi[:], ai[:], pc[:, 0, :], ALU.mult)
    kb.tt([ar1, pc], [t1], t1[:], ar1[:], pc[:, 1, :], ALU.mult)
    kb.tt([fi, t1], [fi], fi[:], fi[:], t1[:], ALU.subtract)
    kb.tt([fi, den], [fi], fi[:], fi[:], den[:], ALU.mult)
    bbr, bbi = tmp(n, "bbr"), tmp(n, "bbi")
    kb.tt([fr, pc], [bbr], bbr[:], fr[:], pc[:, 3, :], ALU.mult)
    kb.tt([fi, pc], [t1], t1[:], fi[:], pc[:, 4, :], ALU.mult)
    kb.tt([bbr, t1], [bbr], bbr[:], bbr[:], t1[:], ALU.subtract)
    kb.tt([fr, pc], [bbi], bbi[:], fr[:], pc[:, 4, :], ALU.mult)
    kb.tt([fi, pc], [t1], t1[:], fi[:], pc[:, 3, :], ALU.mult)
    kb.tt([bbi, t1], [bbi], bbi[:], bbi[:], t1[:], ALU.add)
    WB = kb.sb(st, [128, 2, 16, 128], BF16, "WB")
    for ri, src in ((0, bbr), (1, bbi)):
        for d in range(2):
            for q in range(8):
                m = q // 4
                for gl in range(2):
                    o = (m * 2 + d) * 64
                    kb.ts([src, cst], [WB], WB[:, ri, d * 8 + q, gl * 64:(gl + 1) * 64], src[:, o:o + 64],
                          cst[:, 256 + q * 2 + gl:256 + q * 2 + gl + 1], ALU.mult)
    n2 = 16
    dl2, rdec, th, e_s, e_c, th512, k2, r2 = [tmp(n2, x) for x in ("dl2", "rdec", "th", "es", "ec", "th512", "k2", "r2")]
    kb.act([pss], [dl2], dl2[:], pss[:, 2, :], AF.Exp)
    kb.tt([pss, dl2], [rdec], rdec[:], pss[:, 0, :], dl2[:], ALU.mult)
    kb.act([rdec], [rdec], rdec[:], rdec[:], AF.Exp)
    kb.tt([pss, dl2], [th], th[:], pss[:, 1, :], dl2[:], ALU.mult)
    kb.ts([th], [th512], th512[:], th[:], 512.0, ALU.mult)
    sincos(kb, nc, th512, th512[:], n2, e_s, e_s[:], e_c, e_c[:], k2, r2)

    tc_ = kb.sb(st, [128, 8, TB], F32, "tcos")
    ts_ = kb.sb(st, [128, 8, TB], F32, "tsin")
    argt = kb.sb(st, [128, TB], F32, "argt")
    kk = kb.sb(st, [128, TB], F32, "kk")
    rr = kb.sb(st, [128, TB], F32, "rr")
    w1 = [kb.sb(st, [128, TB], F32, "w1") for _ in range(2)]
    w2 = [kb.sb(st, [128, TB], F32, "w2") for _ in range(2)]
    xr_ = [kb.sb(st, [128, TB], F32, "xpr") for _ in range(2)]
    xi_ = [kb.sb(st, [128, TB], F32, "xpi") for _ in range(2)]
    Gr = [kb.sb(st, [128, TB], F32, "Gr") for _ in range(2)]
    Gi = [kb.sb(st, [128, TB], F32, "Gi") for _ in range(2)]
    sr = [kb.sb(st, [128, TB], BF16, "sr") for _ in range(8)]
    si = [kb.sb(st, [128, TB], BF16, "si") for _ in range(8)]
    car = kb.sb(st, [128, 8, 4], F32, "car")
    it = 0
    for d in range(2):
        for q in range(8):
            kb.ts([cst, th], [argt], argt[:], jrow, th[:, d * 8 + q:d * 8 + q + 1], ALU.mult)
            sincos(kb, nc, argt, argt[:], TB, ts_, ts_[:, q, :], tc_, tc_[:, q, :], kk, rr)
        kb.memset([car], car[:], 0.0)
        for c in range(NB):
            tb = c if d == 0 else NB - 1 - c
            cs = slice(tb * TB, (tb + 1) * TB)

            def rv(ap):
                return ap if d == 0 else ap[:, ::-1]

            for q in range(8):
                m = q // 4
                dq = d * 8 + q
                pr_ = psp[(2 * it) % 4]
                pi_ = psp[(2 * it + 1) % 4]
                kb.mmg(pr_, pr_[:], [(WB[:, 0, dq, :], ub[:, m, cs])], [WB, ub])
                kb.mmg(pi_, pi_[:], [(WB[:, 1, dq, :], ub[:, m, cs])], [WB, ub])
                a1, a2 = w1[it % 2], w2[it % 2]
                xr, xi = xr_[it % 2], xi_[it % 2]
                gr, gi = Gr[it % 2], Gi[it % 2]
                C, S = tc_[:, q, :], ts_[:, q, :]
                kb.tt([pr_, tc_], [a1], a1[:], rv(pr_[:]), C, ALU.mult)
                kb.tt([pi_, ts_], [a2], a2[:], rv(pi_[:]), S, ALU.mult)
                kb.tt([a1, a2], [xr], xr[:], a1[:], a2[:], ALU.add)
                kb.tt([pi_, tc_], [a1], a1[:], rv(pi_[:]), C, ALU.mult)
                kb.tt([pr_, ts_], [a2], a2[:], rv(pr_[:]), S, ALU.mult)
                kb.tt([a1, a2], [xi], xi[:], a1[:], a2[:], ALU.subtract)
                dec = rdec[:, dq:dq + 1].to_broadcast([128, TB])
                kb.op("dve", [rdec, xr, car], [gr], lambda: nc.vector.tensor_tensor_scan(
                    out=gr[:], data0=dec, data1=xr[:], initial=car[:, q, 0:1], op0=ALU.mult, op1=ALU.add))
                kb.op("dve", [rdec, xi, car], [gi], lambda: nc.vector.tensor_tensor_scan(
                    out=gi[:], data0=dec, data1=xi[:], initial=car[:, q, 1:2], op0=ALU.mult, op1=ALU.add))
                kb.ts([gi, e_s], [car], car[:, q, 2:3], gi[:, TB - 1:TB], e_s[:, dq:dq + 1], ALU.mult)
                kb.ts([gi, e_c], [car], car[:, q, 3:4], gi[:, TB - 1:TB], e_c[:, dq:dq + 1], ALU.mult)
                kb.stt([gr, e_c, car], [car], car[:, q, 0:1], gr[:, TB - 1:TB], e_c[:, dq:dq + 1], car[:, q, 2:3], ALU.mult, ALU.subtract)
                kb.stt([gr, e_s, car], [car], car[:, q, 1:2], gr[:, TB - 1:TB], e_s[:, dq:dq + 1], car[:, q, 3:4], ALU.mult, ALU.add)
                s_r, s_i = sr[q], si[q]
                kb.tt([gr, tc_], [a1], a1[:], gr[:], C, ALU.mult)
                kb.tt([gi, ts_], [a2], a2[:], gi[:], S, ALU.mult)
                kb.tt([a1, a2], [s_r], rv(s_r[:]), a1[:], a2[:], ALU.subtract)
                kb.tt([gr, ts_], [a1], a1[:], gr[:], S, ALU.mult)
                kb.tt([gi, tc_], [a2], a2[:], gi[:], C, ALU.mult)
                kb.tt([a1, a2], [s_i], rv(s_i[:]), a1[:], a2[:], ALU.add)
                it += 1
                if q % 4 == 3:
                    py = psp[4 + m]
                    terms = []
                    rd = [wcf]
                    for q2 in range(4 * m, 4 * m + 4):
                        terms.append((wcf[:, 0, d * 8 + q2, :], sr[q2][:]))
                        terms.append((wcf[:, 1, d * 8 + q2, :], si[q2][:]))
                        rd += [sr[q2], si[q2]]
                    kb.mmg(py, py[:], terms, rd)
                    if d == 0:
                        kb.act([py], [Y], Y[:, m, cs], py[:], AF.Copy)
                    else:
                        kb.tt([py, Y], [Y], Y[:, m, cs], py[:], Y[:, m, cs], ALU.add)
    kb.phase_end()
    st.close()
    st = st_outer
    yb = kb.sb(st, [128, 2, TB], F32, "yb")
    wk = kb.sb(st, [128, 2, TB], F32, "wk")
    geb = kb.sb(st, [128, 2, TB], BF16, "geb")
    gef = kb.sb(st, [128, 2, TB], F32, "gef")
    sgl = kb.sb(st, [128, 2, TB], F32, "sgl")
    sq = kb.sb(st, [128, 2, TB], BF16, "sq")
    ob = kb.sb(st, [128, 2, TB], BF16, "ob")
    scr = kb.sb(st, [128, TB], F32, "scr")
    rstd = kb.sb(st, [128, TB], F32, "rstd")
    GC = 2.0 * math.sqrt(2.0 / math.pi)
    for tb in range(NB):
        cs = slice(tb * TB, (tb + 1) * TB)
        for m in range(2):
            kb.stt([ub, vv, Y], [yb], yb[:, m, :], ub[:, m, cs], vv[:, V_S5D + m:V_S5D + m + 1], Y[:, m, cs], ALU.mult, ALU.add)
            kb.tt([yb], [wk], wk[:, m, :], yb[:, m, :], yb[:, m, :], ALU.mult)
            kb.ts([wk], [wk], wk[:, m, :], wk[:, m, :], 0.044715, ALU.mult, 1.0, ALU.add)
            kb.tt([wk, yb], [wk], wk[:, m, :], wk[:, m, :], yb[:, m, :], ALU.mult)
            kb.act([wk], [wk], wk[:, m, :], wk[:, m, :], AF.Sigmoid, scale=GC)
            kb.tt([wk, yb], [gef], gef[:, m, :], wk[:, m, :], yb[:, m, :], ALU.mult)
            kb.op("dve", [gef], [geb], lambda: nc.vector.tensor_copy(out=geb[:, m, :], in_=gef[:, m, :]))
        for oc in range(2):
            p = nps(4)
            kb.mmg(p, p[:], [(Wg[:, kc, oc * 128:(oc + 1) * 128], geb[:, kc, :]) for kc in range(2)], [Wg, geb])
            kb.act([p, vv], [sgl], sgl[:, oc, :], p[:], AF.Sigmoid, bias=vv[:, V_BGLU + oc:V_BGLU + oc + 1])
            kb.tt([sgl, gef], [sgl], sgl[:, oc, :], sgl[:, oc, :], gef[:, oc, :], ALU.mult)
        group_norm_store(st, sgl, [sgl[:, 0, :], sgl[:, 1, :]], vv, V_MIX, 0, cs, sq, scr, rstd, ob)


def emit_s5h(kb, nc, st, l, vv, cst, jrow, s5s_d, s5b_d, cst2_d, w_glu, uT, psp, nps, group_norm_store, onesb):
    ident = cst[:, 0:128]
    st_outer = st
    Wg = kb.sb(st_outer, [128, 2, 256], BF16, "Wg")
    for kc in range(2):
        kb.dma("pool", [], [Wg], Wg[:, kc, :], w_glu[l, kc * 128:(kc + 1) * 128, :])
    ub = kb.sb(st_outer, [128, 2, L], BF16, "ub")
    kb.dma("sp", [], [ub], ub[:], uT.rearrange("(m p) t -> p m t", p=128))
    selb = kb.sb(st_outer, [128, 8, 240], BF16, "selb")
    kb.dma("pool", [], [selb], selb[:], cst2_d[:, 0:1920].rearrange("p (a b) -> p a b", a=8))
    Mg = kb.sb(st_outer, [128, 16, 128], BF16, "Mg")
    WXp = kb.sb(st_outer, [128, 64, 128], BF16, "WXp")
    WIp = kb.sb(st_outer, [128, 64, 128], BF16, "WIp")
    kb.memset([WXp], WXp[:], 0.0)
    kb.memset([WIp], WIp[:], 0.0, e="pool")
    r8 = kb.sb(st_outer, [128, 16], F32, "r8")
    th8 = kb.sb(st_outer, [128, 16], F32, "th8")
    st = contextlib.ExitStack()
    c2 = kb.sb(st, [128, 288], F32, "c2")
    kb.dma("sp", [], [c2], c2[:], cst2_d[:, 1920:2208])
    maskL, maskU = c2[:, 0:128], c2[:, 128:256]
    pss = kb.sb(st, [128, 3, 16], F32, "s5s")
    kb.dma("sp", [], [pss], pss[:], s5s_d[l].rearrange("p (a b) -> p a b", a=3))
    BC = kb.sb(st, [128, 4, 16, 16], F32, "BC")
    kb.dma("sp", [], [BC], BC[:], s5b_d[l].rearrange("p (a b c) -> p a b c", a=4, b=16))

    def tmp(n, nm):
        return kb.sb(st, [128, n], F32, nm)

    n = 16
    dl, are, aim, mag, sn, cs_, k_t, r_t = [tmp(n, x) for x in ("dl", "are", "aim", "mag", "sn", "cs", "kt", "rt")]
    kb.act([pss], [dl], dl[:], pss[:, 2, :], AF.Exp)
    kb.tt([pss, dl], [are], are[:], pss[:, 0, :], dl[:], ALU.mult)
    kb.tt([pss, dl], [aim], aim[:], pss[:, 1, :], dl[:], ALU.mult)
    kb.act([are], [mag], mag[:], are[:], AF.Exp)
    sincos(kb, nc, aim, aim[:], n, sn, sn[:], cs_, cs_[:], k_t, r_t)
    ar1, ai, den, fr, fi, t1 = [tmp(n, x) for x in ("ar1", "ai", "den", "fr", "fi", "t1")]
    kb.tt([mag, cs_], [ar1], ar1[:], mag[:], cs_[:], ALU.mult)
    kb.ts([ar1], [ar1], ar1[:], ar1[:], -1.0, ALU.add)
    kb.tt([mag, sn], [ai], ai[:], mag[:], sn[:], ALU.mult)
    kb.tt([pss], [den], den[:], pss[:, 0, :], pss[:, 0, :], ALU.mult)
    kb.tt([pss], [t1], t1[:], pss[:, 1, :], pss[:, 1, :], ALU.mult)
    kb.tt([den, t1], [den], den[:], den[:], t1[:], ALU.add)
    kb.recip([den], [den], den[:], den[:])
    kb.tt([ar1, pss], [fr], fr[:], ar1[:], pss[:, 0, :], ALU.mult)
    kb.tt([ai, pss], [t1], t1[:], ai[:], pss[:, 1, :], ALU.mult)
    kb.tt([fr, t1], [fr], fr[:], fr[:], t1[:], ALU.add)
    kb.tt([fr, den], [fr], fr[:], fr[:], den[:], ALU.mult)
    kb.tt([ai, pss], [fi], fi[:], ai[:], pss[:, 0, :], ALU.mult)
    kb.tt([ar1, pss], [t1], t1[:], ar1[:], pss[:, 1, :], ALU.mult)
    kb.tt([fi, t1], [fi], fi[:], fi[:], t1[:], ALU.subtract)
    kb.tt([fi, den], [fi], fi[:], fi[:], den[:], ALU.mult)
    bbr = kb.sb(st, [128, 16, 16], F32, "bbr")
    bbi = kb.sb(st, [128, 16, 16], F32, "bbi")
    tb_ = kb.sb(st, [128, 16, 16], F32, "tb_")
    frb = fr[:].unsqueeze(2).to_broadcast([128, 16, 16])
    fib = fi[:].unsqueeze(2).to_broadcast([128, 16, 16])
    kb.tt([fr, BC], [bbr], bbr[:], BC[:, 0], frb, ALU.mult)
    kb.tt([fi, BC], [tb_], tb_[:], BC[:, 1], fib, ALU.mult)
    kb.tt([bbr, tb_], [bbr], bbr[:], bbr[:], tb_[:], ALU.subtract)
    kb.tt([fr, BC], [bbi], bbi[:], BC[:, 1], frb, ALU.mult)
    kb.tt([fi, BC], [tb_], tb_[:], BC[:, 0], fib, ALU.mult)
    kb.tt([bbi, tb_], [bbi], bbi[:], bbi[:], tb_[:], ALU.add)
    PW = {}
    for nm, off in (("asc", 256), ("desc", 272)):
        pre = kb.sb(st, [128, 16, 16], F32, "pw_re")
        pim = kb.sb(st, [128, 16, 16], F32, "pw_im")
        are_j = kb.sb(st, [128, 16, 16], F32, "are_j")
        aim_j = kb.sb(st, [128, 16, 16], F32, "aim_j")
        for dq in range(16):
            kb.ts([c2, are], [are_j], are_j[:, dq, :], c2[:, off:off + 16], are[:, dq:dq + 1], ALU.mult)
            kb.ts([c2, aim], [aim_j], aim_j[:, dq, :], c2[:, off:off + 16], aim[:, dq:dq + 1], ALU.mult)
        mg_ = tmp(256, "mgj")
        sj = tmp(256, "sj")
        cj = tmp(256, "cj")
        kj = tmp(256, "kj")
        rj = tmp(256, "rj")
        kb.act([are_j], [mg_], mg_[:], are_j[:].rearrange("p a b -> p (a b)"), AF.Exp)
        sincos(kb, nc, aim_j, aim_j[:].rearrange("p a b -> p (a b)"), 256, sj, sj[:], cj, cj[:], kj, rj)
        kb.tt([mg_, cj], [pre], pre[:].rearrange("p a b -> p (a b)"), mg_[:], cj[:], ALU.mult)
        kb.tt([mg_, sj], [pim], pim[:].rearrange("p a b -> p (a b)"), mg_[:], sj[:], ALU.mult)
        PW[nm] = (pre, pim)
    kb.ts([are], [r8], r8[:], are[:], 8.0, ALU.mult)
    kb.act([r8], [r8], r8[:], r8[:], AF.Exp)
    kb.ts([aim], [th8], th8[:], aim[:], 8.0, ALU.mult)

    def cprod(d, Zr, Zi, Zt, tab, j0, nm):
        Pr, Pi = PW[tab]
        Tr = kb.sb(st, [128, 8, 8, 16], F32, nm + "r")
        Ti = kb.sb(st, [128, 8, 8, 16], F32, nm + "i")
        Tt = kb.sb(st, [128, 8, 8, 16], F32, nm + "t")
        sh_ = [128, 8, 8, 16]
        zr = Zr[:, d * 8:(d + 1) * 8, :].unsqueeze(2).to_broadcast(sh_)
        zi = Zi[:, d * 8:(d + 1) * 8, :].unsqueeze(2).to_broadcast(sh_)
        pr = Pr[:, d * 8:(d + 1) * 8, j0:j0 + 8].unsqueeze(3).to_broadcast(sh_)
        pi_ = Pi[:, d * 8:(d + 1) * 8, j0:j0 + 8].unsqueeze(3).to_broadcast(sh_)
        kb.tt(Zt + [Pr], [Tr], Tr[:], zr, pr, ALU.mult)
        kb.tt(Zt + [Pi], [Tt], Tt[:], zi, pi_, ALU.mult)
        kb.tt([Tr, Tt], [Tr], Tr[:], Tr[:], Tt[:], ALU.subtract)
        kb.tt(Zt + [Pi], [Ti], Ti[:], zr, pi_, ALU.mult)
        kb.tt(Zt + [Pr], [Tt], Tt[:], zi, pr, ALU.mult)
        kb.tt([Ti, Tt], [Ti], Ti[:], Ti[:], Tt[:], ALU.add)
        return Tr, Ti

    Bz = (bbr, bbi, [bbr, bbi])
    Cr_, Ci_ = BC[:, 2], BC[:, 3]

    class _V:
        def __init__(self, ap):
            self.ap = ap

        def __getitem__(self, k):
            return self.ap[k]

    Cz = (_V(Cr_), _V(Ci_), [BC])
    prods = {}
    prods["L", 0] = cprod(0, *Bz, "desc", 8, "Lf")
    prods["R", 0] = cprod(0, *Cz, "asc", 7, "Rf")
    prods["WI", 0] = cprod(0, *Cz, "asc", 8, "WIf")
    prods["WX", 0] = cprod(0, *Bz, "desc", 1, "WXf")
    prods["L", 1] = cprod(1, *Bz, "asc", 7, "Lb")
    prods["R", 1] = cprod(1, *Cz, "desc", 8, "Rb")
    prods["WI", 1] = cprod(1, *Cz, "desc", 0, "WIb")
    prods["WX", 1] = prods["L", 1]
    pp_i = 0
    for d in range(2):
        Lr, Li = prods["L", d]
        nLi = kb.sb(st, [128, 8, 8, 16], F32, "nLi")
        kb.ts([Li], [nLi], nLi[:], Li[:], -1.0, ALU.mult)
        prods["nLi", d] = nLi
        WIr, WIi = prods["WI", d]
        WXr, WXi = prods["WX", d]
        for q in range(8):
            for gl in range(2):
                g = 2 * q + gl
                rows = slice(gl * 64, (gl + 1) * 64)
                for ri, src, sgn in ((0, WIr, 1.0), (1, WIi, -1.0)):
                    idx = (d * 16 + g) * 2 + ri
                    kb.act([src], [WIp], WIp[rows, idx, :], src[rows, q].rearrange("p a b -> p (a b)"), AF.Copy, scale=sgn)
            for ri, src in ((0, WXr), (1, WXi)):
                pt_ = psp[pp_i % 4]
                pp_i += 1
                kb.op("pe", [src, cst], [pt_], lambda: nc.tensor.transpose(
                    out=pt_[:, 0:128], in_=src[:, q].rearrange("p a b -> p (a b)"), identity=ident))
                for gl in range(2):
                    idx = (d * 16 + 2 * q + gl) * 2 + ri
                    kb.act([pt_], [WXp], WXp[:, idx, gl * 64:(gl + 1) * 64], pt_[:, gl * 64:(gl + 1) * 64], AF.Copy)
    for g in range(16):
        q, gl = g // 2, g % 2
        rows = slice(gl * 64, (gl + 1) * 64)
        pm = []
        for d in range(2):
            Lr, Li = prods["L", d]
            nLi = prods["nLi", d]
            Rr, Ri = prods["R", d]
            pt_ = psp[4 + (pp_i % 4)]
            pp_i += 1
            f2 = lambda t_: t_[rows, q].rearrange("p a b -> p (a b)")
            kb.mmg(pt_, pt_[:, 0:128], [(f2(Lr), f2(Rr)), (f2(nLi), f2(Ri))], [Lr, nLi, Rr, Ri])
            pm.append(pt_)
        tm = kb.sb(st, [128, 128], F32, "tm")
        kb.tt([pm[0], c2], [tm], tm[:], pm[0][:, 0:128], maskL, ALU.mult)
        tm2 = kb.sb(st, [128, 128], F32, "tm2")
        kb.tt([pm[1], c2], [tm2], tm2[:], pm[1][:, 0:128], maskU, ALU.mult)
        kb.tt([tm, tm2], [Mg], Mg[:, g, :], tm[:], tm2[:], ALU.add)
    kb.phase_end()
    st.close()
    st = contextlib.ExitStack()
    Ug = kb.sb(st, [128, 16, TB], BF16, "Ug")
    Yg = kb.sb(st, [128, 16, TB], BF16, "Yg")
    Sp = kb.sb(st, [128, 32, TB], BF16, "Sp")
    tc_ = kb.sb(st, [128, TB], F32, "tcos")
    ts_ = kb.sb(st, [128, TB], F32, "tsin")
    argt = kb.sb(st, [128, TB], F32, "argt")
    kk = kb.sb(st, [128, TB], F32, "kk")
    rr = kb.sb(st, [128, TB], F32, "rr")
    w1 = [kb.sb(st, [128, TB], F32, "w1") for _ in range(2)]
    w2 = [kb.sb(st, [128, TB], F32, "w2") for _ in range(2)]
    xr_ = [kb.sb(st, [128, TB], F32, "xpr") for _ in range(2)]
    xi_ = [kb.sb(st, [128, TB], F32, "xpi") for _ in range(2)]
    Gr = [kb.sb(st, [128, TB], F32, "Gr") for _ in range(2)]
    Gi = [kb.sb(st, [128, TB], F32, "Gi") for _ in range(2)]
    for g in range(16):
        m, g8 = g // 8, g % 8
        pu = psp[g % 4]
        kb.mmg(pu, pu[:], [(selb[:, g8, (7 - k) * 16:(7 - k) * 16 + 128], ub[:, m, k:L:8]) for k in range(8)], [selb, ub])
        if g % 2 == 0:
            kb.act([pu], [Ug], Ug[:, g, :], pu[:], AF.Copy)
        else:
            kb.op("dve", [pu], [Ug], lambda: nc.vector.tensor_copy(out=Ug[:, g, :], in_=pu[:]))
    it = 0
    for d in range(2):
        def rv(ap):
            return ap if d == 0 else ap[:, ::-1]
        for q in range(8):
            dq = d * 8 + q
            kb.ts([cst, th8], [argt], argt[:], jrow, th8[:, dq:dq + 1], ALU.mult)
            sincos(kb, nc, argt, argt[:], TB, ts_, ts_[:], tc_, tc_[:], kk, rr)
            px = []
            for ri in range(2):
                p_ = psp[(2 * it + ri) % 4]
                kb.mmg(p_, p_[:], [(WXp[:, (d * 16 + 2 * q + gl) * 2 + ri, :], Ug[:, 2 * q + gl, :]) for gl in range(2)], [WXp, Ug])
                px.append(p_)
            pr_, pi_ = px
            a1, a2 = w1[it % 2], w2[it % 2]
            xr, xi = xr_[it % 2], xi_[it % 2]
            gr, gi = Gr[it % 2], Gi[it % 2]
            C, S = tc_[:], ts_[:]
            kb.tt([pr_, tc_], [a1], a1[:], rv(pr_[:]), C, ALU.mult)
            kb.tt([pi_, ts_], [a2], a2[:], rv(pi_[:]), S, ALU.mult)
            kb.tt([a1, a2], [xr], xr[:], a1[:], a2[:], ALU.add)
            kb.tt([pi_, tc_], [a1], a1[:], rv(pi_[:]), C, ALU.mult)
            kb.tt([pr_, ts_], [a2], a2[:], rv(pr_[:]), S, ALU.mult)
            kb.tt([a1, a2], [xi], xi[:], a1[:], a2[:], ALU.subtract)
            dec = r8[:, dq:dq + 1].to_broadcast([128, TB])
            kb.op("dve", [r8, xr], [gr], lambda: nc.vector.tensor_tensor_scan(
                out=gr[:], data0=dec, data1=xr[:], initial=0.0, op0=ALU.mult, op1=ALU.add))
            kb.op("dve", [r8, xi], [gi], lambda: nc.vector.tensor_tensor_scan(
                out=gi[:], data0=dec, data1=xi[:], initial=0.0, op0=ALU.mult, op1=ALU.add))
            n1 = TB - 1
            for ri in range(2):
                o_ = rv(Sp[:, dq * 2 + ri, :])
                kb.memset([Sp], o_[:, 0:1], 0.0, e="pool")
                if ri == 0:
                    kb.tt([gr, tc_], [a1], a1[:, 0:n1], gr[:, 0:n1], C[:, 0:n1], ALU.mult)
                    kb.tt([gi, ts_], [a2], a2[:, 0:n1], gi[:, 0:n1], S[:, 0:n1], ALU.mult)
                    kb.tt([a1, a2], [Sp], o_[:, 1:TB], a1[:, 0:n1], a2[:, 0:n1], ALU.subtract)
                else:
                    kb.tt([gr, ts_], [a1], a1[:, 0:n1], gr[:, 0:n1], S[:, 0:n1], ALU.mult)
                    kb.tt([gi, tc_], [a2], a2[:, 0:n1], gi[:, 0:n1], C[:, 0:n1], ALU.mult)
                    kb.tt([a1, a2], [Sp], o_[:, 1:TB], a1[:, 0:n1], a2[:, 0:n1], ALU.add)
            it += 1
    for g in range(16):
        q = g // 2
        py = psp[4 + (g % 4)]
        terms = [(Mg[:, g, :], Ug[:, g, :])]
        for d in range(2):
            for ri in range(2):
                terms.append((WIp[:, (d * 16 + g) * 2 + ri, :], Sp[:, (d * 8 + q) * 2 + ri, :]))
        kb.mmg(py, py[:], terms, [Mg, Ug, WIp, Sp])
        if g % 2 == 0:
            kb.act([py], [Yg], Yg[:, g, :], py[:], AF.Copy)
        else:
            kb.op("dve", [py], [Yg], lambda: nc.vector.tensor_copy(out=Yg[:, g, :], in_=py[:]))
    yb = kb.sb(st, [128, 2, TB], F32, "yb")
    wk = kb.sb(st, [128, 2, TB], F32, "wk")
    geb = kb.sb(st, [128, 2, TB], BF16, "geb")
    gef = kb.sb(st, [128, 2, TB], F32, "gef")
    sgl = kb.sb(st, [128, 2, TB], F32, "sgl")
    sq = kb.sb(st, [128, 2, TB], BF16, "sq")
    ob = kb.sb(st, [128, 2, TB], BF16, "ob")
    scr = kb.sb(st, [128, TB], F32, "scr")
    rstd = kb.sb(st, [128, TB], F32, "rstd")
    GC = 2.0 * math.sqrt(2.0 / math.pi)
    for tb in range(NB):
        cs = slice(tb * TB, (tb + 1) * TB)
        for m in range(2):
            pyb = psp[(2 * tb + m) % 4]
            kb.wait("pe", kb.deps([selb, Yg], [pyb]))
            inst = None
            for t in range(8):
                for g8 in range(8):
                    inst = nc.tensor.matmul(pyb[:][:, t:TB:8], selb[:, t, (7 - g8) * 16:(7 - g8) * 16 + 128],
                                            Yg[:, 8 * m + g8, tb * 64:(tb + 1) * 64], start=(g8 == 0), stop=(g8 == 7))
            kb.cnt["pe"] += 1
            inst.then_inc(kb.sem["pe"], 1)
            tok = (kb.sem["pe"], kb.cnt["pe"])
            kb.fin(tok, [selb, Yg], [pyb])
            kb.stt([ub, vv, pyb], [yb], yb[:, m, :], ub[:, m, cs], vv[:, V_S5D + m:V_S5D + m + 1], pyb[:], ALU.mult, ALU.add)
            kb.tt([yb], [wk], wk[:, m, :], yb[:, m, :], yb[:, m, :], ALU.mult)
            kb.ts([wk], [wk], wk[:, m, :], wk[:, m, :], 0.044715, ALU.mult, 1.0, ALU.add)
            kb.tt([wk, yb], [wk], wk[:, m, :], wk[:, m, :], yb[:, m, :], ALU.mult)
            kb.act([wk], [wk], wk[:, m, :], wk[:, m, :], AF.Sigmoid, scale=GC)
            kb.tt([wk, yb], [gef], gef[:, m, :], wk[:, m, :], yb[:, m, :], ALU.mult)
            kb.op("dve", [gef], [geb], lambda: nc.vector.tensor_copy(out=geb[:, m, :], in_=gef[:, m, :]))
        for oc in range(2):
            p = psp[4 + (oc % 2)]
            kb.mmg(p, p[:], [(Wg[:, kc, oc * 128:(oc + 1) * 128], geb[:, kc, :]) for kc in range(2)], [Wg, geb])
            kb.act([p, vv], [sgl], sgl[:, oc, :], p[:], AF.Sigmoid, bias=vv[:, V_BGLU + oc:V_BGLU + oc + 1])
            kb.tt([sgl, gef], [sgl], sgl[:, oc, :], sgl[:, oc, :], gef[:, oc, :], ALU.mult)
        group_norm_store(st, sgl, [sgl[:, 0, :], sgl[:, 1, :]], vv, V_MIX, 0, cs, sq, scr, rstd, ob)
    kb.phase_end()
    st.close()


_NC_CACHE = {}


def kernel(**inputs):
    inp = {k: np.asarray(v) for k, v in inputs.items()}
    sh, xs = host_prep(inp)
    if "nc" not in _NC_CACHE:
        _NC_CACHE["nc"] = build()
    nc = _NC_CACHE["nc"]
    n = len(xs)
    in_maps = []
    for b in range(n):
        m = dict(sh)
        m["xT"] = xs[b]
        in_maps.append(m)
    res = run_bass_kernel_spmd(nc, in_maps, core_ids=list(range(n)))
    out = np.stack([np.ascontiguousarray(res.results[b]["outT"].T) for b in range(n)])
    return out.astype(np.float32)
```

```python
import contextlib
import math
import numpy as np
import concourse.bass as bass
import concourse.mybir as mybir
from concourse.bass_utils import run_bass_kernel_spmd

F32 = mybir.dt.float32
BF16 = mybir.dt.bfloat16
AF = mybir.ActivationFunctionType
ALU = mybir.AluOpType

L = 4096
D = 1024
NB = 8
TB = 512
DEPTH = 2
DFF = 2816
NCOL = 2112
EPS = 1e-6
TWO_PI = 2.0 * math.pi
MAGIC = 12582912.0


class Tl:
    def __init__(self, h):
        self.h = h
        self.w = {}
        self.r = {}

    def __getitem__(self, k):
        return self.h[k]


class TlV(Tl):
    def __init__(self, h, c0, n):
        self.h = h
        self.c0 = c0
        self.n = n
        self.w = {}
        self.r = {}

    def __getitem__(self, k):
        if not isinstance(k, tuple):
            k = (k, slice(None))
        rows, cols = k
        a, b, stp = cols.indices(self.n)
        assert stp == 1
        return self.h[rows, self.c0 + a:self.c0 + b]


class KB:
    def __init__(self, nc):
        self.nc = nc
        self.eng = {"pe": nc.tensor, "act": nc.scalar, "dve": nc.vector, "pool": nc.gpsimd, "sp": nc.sync}
        self.sem = {}
        self.cnt = {}
        self.seen = {e: {} for e in self.eng}
        self.stack = contextlib.ExitStack()
        for e in ("pe", "act", "dve", "pool"):
            self.sem[e] = self.stack.enter_context(nc.semaphore("prog_" + e))
            self.cnt[e] = 0
        self.dsem = [self.stack.enter_context(nc.semaphore(f"dma{i}")) for i in range(16)]
        self.dcnt = [0] * 16
        self.dnext = {"sp": 0, "pool": 0, "act": 0}
        self.uid = 0

    def name(self, p):
        self.uid += 1
        return f"{p}_{self.uid}"

    def sb(self, st, shape, dt, nm="t"):
        return Tl(st.enter_context(self.nc.sbuf_tensor(self.name(nm), list(shape), dt)))

    def ps(self, st, shape, dt=F32, nm="p"):
        return Tl(st.enter_context(self.nc.psum_tensor(self.name(nm), list(shape), dt)))

    def wait(self, e, toks):
        seen = self.seen[e]
        need = {}
        for (s, v) in toks:
            if seen.get(s, 0) < v and need.get(s, 0) < v:
                need[s] = v
        for s, v in need.items():
            self.eng[e].wait_ge(s, v)
            seen[s] = v

    def deps(self, reads, writes, part=False):
        toks = []
        for t in reads:
            toks.extend(t.w.items())
        for t in writes:
            if not part:
                toks.extend(t.w.items())
            toks.extend(t.r.items())
        return toks

    def fin(self, tok, reads, writes, part=False):
        for t in reads:
            if t.r.get(tok[0], 0) < tok[1]:
                t.r[tok[0]] = tok[1]
        for t in writes:
            if part:
                if t.w.get(tok[0], 0) < tok[1]:
                    t.w[tok[0]] = tok[1]
            else:
                t.w = {tok[0]: tok[1]}
                t.r = {}

    def op(self, e, reads, writes, fn):
        self.wait(e, self.deps(reads, writes))
        inst = fn()
        self.cnt[e] += 1
        inst.then_inc(self.sem[e], 1)
        tok = (self.sem[e], self.cnt[e])
        self.fin(tok, reads, writes)
        return tok

    def dma(self, q, reads, writes, out, in_, part=False):
        self.wait(q, self.deps(reads, writes, part))
        base, npool = {"sp": (0, 6), "pool": (6, 6), "act": (12, 4)}[q]
        i = base + self.dnext[q]
        self.dnext[q] = (self.dnext[q] + 1) % npool
        inst = self.eng[q].dma_start(out=out, in_=in_)
        self.dcnt[i] += 16
        inst.then_inc(self.dsem[i], 16)
        tok = (self.dsem[i], self.dcnt[i])
        self.fin(tok, reads, writes, part)
        return tok

    def mmg(self, out_t, out_ap, terms, reads):
        self.wait("pe", self.deps(reads, [out_t]))
        n = len(terms)
        inst = None
        for i, (l, r) in enumerate(terms):
            inst = self.nc.tensor.matmul(out_ap, l, r, start=(i == 0), stop=(i == n - 1))
        self.cnt["pe"] += 1
        inst.then_inc(self.sem["pe"], 1)
        tok = (self.sem["pe"], self.cnt["pe"])
        self.fin(tok, reads, [out_t])
        return tok

    def mm_acc(self, out_t, out_ap, l, r, reads, start, stop, first):
        d = self.deps(reads, [out_t] if first else [])
        self.wait("pe", d)
        inst = self.nc.tensor.matmul(out_ap, l, r, start=start, stop=stop)
        self.cnt["pe"] += 1
        inst.then_inc(self.sem["pe"], 1)
        tok = (self.sem["pe"], self.cnt["pe"])
        for t in reads:
            if t.r.get(tok[0], 0) < tok[1]:
                t.r[tok[0]] = tok[1]
        if first:
            out_t.r = {}
        out_t.w = {tok[0]: tok[1]}
        return tok

    def phase_end(self):
        nc = self.nc
        toks = [(self.dsem[i], self.dcnt[i]) for i in range(len(self.dsem)) if self.dcnt[i] > 0]
        toks += [(self.sem[e], self.cnt[e]) for e in self.sem if self.cnt[e] > 0]
        for e in self.eng:
            self.wait(e, toks)
        nc.all_engine_barrier()

    def act(self, reads, writes, out, in_, func, scale=1.0, bias=None):
        if bias is None:
            return self.op("act", reads, writes, lambda: self.nc.scalar.activation(out=out, in_=in_, func=func, scale=scale))
        return self.op("act", reads, writes,
                       lambda: self.nc.scalar.activation(out=out, in_=in_, func=func, scale=scale, bias=bias))

    def tt(self, reads, writes, out, a, b, op, e="dve"):
        return self.op(e, reads, writes, lambda: self.eng[e].tensor_tensor(out=out, in0=a, in1=b, op=op))

    def ts(self, reads, writes, out, a, s1, op0, s2=None, op1=None, e="dve"):
        if op1 is None:
            return self.op(e, reads, writes, lambda: self.eng[e].tensor_scalar(out=out, in0=a, scalar1=s1, scalar2=None, op0=op0))
        return self.op(e, reads, writes,
                       lambda: self.eng[e].tensor_scalar(out=out, in0=a, scalar1=s1, scalar2=s2, op0=op0, op1=op1))

    def stt(self, reads, writes, out, a, s, b, op0, op1):
        return self.op("dve", reads, writes,
                       lambda: self.nc.vector.scalar_tensor_tensor(out=out, in0=a, scalar=s, in1=b, op0=op0, op1=op1))

    def recip(self, reads, writes, out, in_):
        return self.op("dve", reads, writes, lambda: self.nc.vector.reciprocal(out=out, in_=in_))

    def memset(self, writes, ap, val, e="dve"):
        return self.op(e, [], writes, lambda: self.eng[e].memset(ap, val))


def rope_swap_idx(r):
    m = r // 4
    idx = np.arange(r)
    q = idx // m
    return np.where(q % 2 == 0, idx + m, idx - m)


def rope_tables(r):
    m = r // 4
    inv = (10000.0 ** (-np.arange(m, dtype=np.float32) / m)).astype(np.float32)
    t = np.arange(L)
    row = (t // 64).astype(np.float32)
    col = (t % 64).astype(np.float32)
    cos = np.zeros((r, L), np.float32)
    sin = np.zeros((r, L), np.float32)
    for d in range(r):
        qd, f = d // m, d % m
        pos = row if qd < 2 else col
        ang = (pos * inv[f]).astype(np.float32)
        cos[d] = np.cos(ang).astype(np.float32)
        s = np.sin(ang).astype(np.float32)
        sin[d] = -s if qd % 2 == 0 else s
    return cos, sin


def host_prep(inp):
    f = np.float32
    sh = {}
    swg = rope_swap_idx(64)
    swm = rope_swap_idx(32)
    w_in = inp["w_in"]
    cols = []
    cols.append(np.arange(0, 768))
    cols.append(np.arange(768, 1024))
    cols.append(np.arange(1024, 1152))
    qg0 = 1184
    cols.append(np.arange(qg0, qg0 + 256))
    cols.append(np.concatenate([qg0 + h * 64 + swg for h in range(4)]))
    kg0 = 1440
    cols.append(np.arange(kg0, kg0 + 128))
    cols.append(np.concatenate([kg0 + h * 64 + swg for h in range(2)]))
    cols.append(np.arange(1568, 1696))
    cols.append(np.arange(1152, 1184))
    cols.append(1152 + swm)
    cols = np.concatenate(cols)
    assert cols.shape[0] == NCOL
    sh["w_in"] = np.ascontiguousarray(w_in[:, :, cols]).astype(f)
    uq = []
    for h in range(4):
        uq.append(np.arange(h * 96, h * 96 + 64))
    for h in range(4):
        uq.append(h * 96 + 64 + np.arange(32))
    for h in range(4):
        uq.append(h * 96 + 64 + swm)
    uq = np.concatenate(uq)
    sh["w_uq"] = np.ascontiguousarray(inp["mla_w_uq"][:, :, uq]).astype(f)
    ukv = []
    for h in range(4):
        ukv.append(np.arange(h * 128, h * 128 + 64))
    for h in range(4):
        ukv.append(np.arange(h * 128 + 64, h * 128 + 128))
    ukv = np.concatenate(ukv)
    sh["w_ukv"] = np.ascontiguousarray(inp["mla_w_ukv"][:, :, ukv]).astype(f)
    sh["w_glu"] = inp["s5_w_glu"].astype(f)
    sh["w_out"] = inp["w_out"].astype(f)
    sh["w_up"] = inp["ffn_w_up"].astype(f)
    sh["w_down"] = inp["ffn_w_down"].astype(f)

    def pp(v, n):
        return np.ascontiguousarray(v.reshape(n, 128).T)

    vec = []
    for l in range(DEPTH):
        c = []
        c.append(pp(inp["attn_norm_g"][l], 8))
        c.append(pp(inp["mla_q_norm_g"][l], 2))
        c.append(pp(inp["mla_kv_norm_g"][l], 1))
        gq = inp["gqa_q_norm_g"][l]
        gk = inp["gqa_k_norm_g"][l]
        c.append(np.tile(gq, 2)[:, None])
        c.append(np.tile(gq[swg], 2)[:, None])
        c.append(np.tile(gk, 2)[:, None])
        c.append(np.tile(gk[swg], 2)[:, None])
        c.append(pp(inp["mix_norm_g"][l].reshape(-1), 8))
        c.append(pp(inp["ffn_norm_g"][l], 8))
        c.append(pp(inp["s5_d"][l], 2))
        c.append(pp(inp["s5_b_glu"][l], 2))
        c.append(pp(inp["conv_dw_b"][l], 2))
        c.append(pp(inp["conv_ln_g"][l], 2))
        c.append(pp(inp["conv_ln_b"][l], 2))
        c.append(pp(inp["final_norm_g"], 8))
        cw = inp["conv_dw_w"][l]
        c.append(np.ascontiguousarray(cw.reshape(31, 2, 128).transpose(2, 1, 0)).reshape(128, 62))
        fw = inp["ffn_conv_w"][l]
        c.append(np.ascontiguousarray(fw.reshape(3, 44, 128).transpose(2, 1, 0)).reshape(128, 132))
        c.append(pp(inp["ffn_conv_b"][l], 44))
        vec.append(np.concatenate(c, axis=1))
    sh["vec"] = np.stack(vec).astype(f)
    s5c = np.zeros((DEPTH, 128, 5, 2, 2, 64), f)
    s5s = np.zeros((DEPTH, 128, 3, 2, 8), f)
    wc = np.zeros((DEPTH, 128, 2, 2, 8, 128), f)
    for l in range(DEPTH):
        for d in range(2):
            for g in range(16):
                m, g8 = g // 8, g % 8
                q, gl = g // 2, g % 2
                for h in range(16):
                    s5c[l, g8 * 16 + h, 0, m, d] = inp["s5_lam_re"][l, d, g]
                    s5c[l, g8 * 16 + h, 1, m, d] = inp["s5_lam_im"][l, d, g]
                    s5c[l, g8 * 16 + h, 2, m, d] = inp["s5_log_step"][l, d, g]
                    s5c[l, g8 * 16 + h, 3, m, d] = inp["s5_b_re"][l, d, g, :, h]
                    s5c[l, g8 * 16 + h, 4, m, d] = inp["s5_b_im"][l, d, g, :, h]
                s5s[l, gl * 64:(gl + 1) * 64, 0, d, q] = inp["s5_lam_re"][l, d, g]
                s5s[l, gl * 64:(gl + 1) * 64, 1, d, q] = inp["s5_lam_im"][l, d, g]
                s5s[l, gl * 64:(gl + 1) * 64, 2, d, q] = inp["s5_log_step"][l, d, g]
                wc[l, gl * 64:(gl + 1) * 64, 0, d, q, g8 * 16:(g8 + 1) * 16] = inp["s5_c_re"][l, d, g].T
                wc[l, gl * 64:(gl + 1) * 64, 1, d, q, g8 * 16:(g8 + 1) * 16] = inp["s5_c_im"][l, d, g].T
    s5b = np.zeros((DEPTH, 128, 4, 16, 16), f)
    for l in range(DEPTH):
        for d in range(2):
            for g in range(16):
                q, gl = g // 2, g % 2
                rows = slice(gl * 64, (gl + 1) * 64)
                s5b[l, rows, 0, d * 8 + q] = inp["s5_b_re"][l, d, g]
                s5b[l, rows, 1, d * 8 + q] = inp["s5_b_im"][l, d, g]
                s5b[l, rows, 2, d * 8 + q] = inp["s5_c_re"][l, d, g].T
                s5b[l, rows, 3, d * 8 + q] = inp["s5_c_im"][l, d, g].T
    sh["s5b"] = s5b.reshape(DEPTH, 128, 1024)
    cst2 = np.zeros((128, 8 * 240 + 256 + 32), f)
    for a in range(8):
        for h in range(16):
            cst2[a * 16 + h, a * 240 + 7 * 16 + h] = 1.0
    for k in range(8):
        for h in range(16):
            for t in range(8):
                if t >= k:
                    cst2[k * 16 + h, 1920 + t * 16: 1920 + (t + 1) * 16] = 1.0
                if k >= t:
                    cst2[k * 16 + h, 2048 + t * 16: 2048 + (t + 1) * 16] = 1.0
    cst2[:, 2176:2192] = np.arange(-7, 9, dtype=f)[None, :]
    cst2[:, 2192:2208] = np.arange(8, -8, -1, dtype=f)[None, :]
    sh["cst2"] = cst2
    sh["s5c"] = s5c.reshape(DEPTH, 128, 5 * 256)
    sh["s5s"] = s5s.reshape(DEPTH, 128, 48)
    sh["wc"] = wc.reshape(DEPTH, 128, 32 * 128)
    cg, sg = rope_tables(64)
    cm, sm = rope_tables(32)
    sh["ropeg"] = np.stack([np.tile(cg, (2, 1)), np.tile(sg, (2, 1))]).astype(f)
    sh["ropem"] = np.stack([np.tile(cm, (4, 1)), np.tile(sm, (4, 1))]).astype(f)
    cst = np.zeros((128, 128 + 128 + 16 + 512), f)
    cst[:, 0:128] = np.eye(128, dtype=f)
    for p in range(128):
        cst[p, 128 + (p // 64) * 64: 128 + (p // 64) * 64 + 64] = 1.0
    for p in range(128):
        g8 = p // 16
        for q in range(8):
            for gl in range(2):
                if g8 == (2 * q + gl) % 8:
                    cst[p, 256 + q * 2 + gl] = 1.0
    cst[:, 272:272 + 512] = np.arange(512, dtype=f)[None, :]
    sh["cst"] = cst
    xs = [np.ascontiguousarray(inp["x"][b].T).astype(f) for b in range(inp["x"].shape[0])]
    return sh, xs


V_ATTN, V_MLAQ, V_MLAKV, V_GQ, V_GQS, V_GK, V_GKS = 0, 8, 10, 11, 12, 13, 14
V_MIX, V_FFN, V_S5D, V_BGLU, V_CVB, V_LNG, V_LNB, V_FIN, V_CVW, V_FCW, V_FCB = 15, 23, 31, 33, 35, 37, 39, 41, 49, 111, 243
NV = 287


def build(stop_after=None, dbg=False):
    nc = bass.Bass("TRN2", target_bir_lowering=False)
    kb = KB(nc)

    def din(name, shape, dt=F32):
        return nc.dram_tensor(name, list(shape), dt, kind="ExternalInput").ap()

    def dscr(name, shape, dt):
        kind = "ExternalOutput" if dbg else "Internal"
        return nc.dram_tensor(name, list(shape), dt, kind=kind).ap()

    xT = din("xT", [D, L])
    w_in = din("w_in", [DEPTH, D, NCOL])
    w_uq = din("w_uq", [DEPTH, 256, 512])
    w_ukv = din("w_ukv", [DEPTH, 128, 512])
    w_glu = din("w_glu", [DEPTH, 256, 256])
    w_out = din("w_out", [DEPTH, D, D])
    w_up = din("w_up", [DEPTH, D, 2 * DFF])
    w_down = din("w_down", [DEPTH, DFF, D])
    vec_d = din("vec", [DEPTH, 128, NV])
    s5c_d = din("s5c", [DEPTH, 128, 1280])
    s5s_d = din("s5s", [DEPTH, 128, 48])
    wc_d = din("wc", [DEPTH, 128, 4096])
    ropeg_d = din("ropeg", [2, 128, L])
    ropem_d = din("ropem", [2, 128, L])
    cst_d = din("cst", [128, 784])
    s5b_d = din("s5b", [DEPTH, 128, 1024])
    cst2_d = din("cst2", [128, 2208])
    outT = nc.dram_tensor("outT", [D, L], F32, kind="ExternalOutput").ap()

    xres = dscr("xres", [D, L], F32)
    x1T = dscr("x1T", [D, L], F32)
    hn2T = dscr("hn2T", [D, L], BF16)
    uT = dscr("uT", [256, L], BF16)
    gT = dscr("gT", [256, L], BF16)
    qm = dscr("qm", [4, 96, L], BF16)
    km = dscr("km", [4, 96, L], BF16)
    vm = dscr("vm", [L, 256], BF16)
    qg = dscr("qg", [4, 64, L], BF16)
    kg = dscr("kg", [2, 64, L], BF16)
    vg = dscr("vg", [L, 128], BF16)
    mixT = dscr("mixT", [D, L], BF16)
    actT = dscr("actT", [DFF, L], BF16)

    gst = contextlib.ExitStack()
    cst = kb.sb(gst, [128, 784], F32, "cst")
    kb.dma("sp", [], [cst], cst[:], cst_d)
    ident = cst[:, 0:128]
    jrow = cst[:, 272:784]
    onesb = kb.sb(gst, [128, 128], BF16, "onesb")
    kb.memset([onesb], onesb[:], 1.0)
    bones = kb.sb(gst, [128, 128], BF16, "bones")
    kb.op("dve", [cst], [bones], lambda: nc.vector.tensor_copy(out=bones[:], in_=cst[:, 128:256]))
    onesf = kb.sb(gst, [128, 128], F32, "onesf")
    kb.memset([onesf], onesf[:], 1.0)
    vecs = []
    for l in range(DEPTH):
        v = kb.sb(gst, [128, NV], F32, "vec")
        kb.dma("sp", [], [v], v[:], vec_d[l])
        vecs.append(v)
    pbig = [kb.ps(gst, [128, 1024], F32, "ps") for _ in range(4)]
    psp = [TlV(pbig[i // 2].h, (i % 2) * 512, 512) for i in range(8)]
    pst = {"i": 0}

    def nps(n=6):
        t = psp[pst["i"] % n]
        pst["i"] += 1
        return t

    def rms_rstd(st_tiles, sq_srcs, nred, out_rstd, scr, bias, scale, ones=None):
        ss = psp[7]
        kb.mmg(ss, ss[:], [((ones or onesb)[:], ap) for (_, ap) in sq_srcs], [t for (t, _) in sq_srcs] + [ones or onesb])
        kb.act([ss, epsb], [scr], scr[:], ss[:], AF.Sqrt, scale=scale, bias=bias)
        kb.recip([scr], [out_rstd], out_rstd[:], scr[:])

    halfpi = kb.sb(gst, [128, 1], F32, "halfpi")
    kb.memset([halfpi], halfpi[:], math.pi / 2.0)
    kb.halfpi = halfpi
    epsb = kb.sb(gst, [128, 2], F32, "epsb")
    kb.memset([epsb], epsb[:, 0:1], EPS)
    kb.memset([epsb], epsb[:, 1:2], 64.0 * EPS)
    eps_ap = epsb[:, 0:1]
    eps64_ap = epsb[:, 1:2]

    def group_norm_store(st, zt, zaps, gv, gcol, row0, cols, sq, scr, rstd, ob):
        for j in range(2):
            kb.act([zt], [sq], sq[:, j, :], zaps[j], AF.Square)
        rms_rstd(None, [(sq, sq[:, 0, :]), (sq, sq[:, 1, :])], 2, rstd, scr, eps_ap, 1.0 / 256.0)
        for j in range(2):
            kb.stt([zt, gv, rstd], [ob], ob[:, j, :], zaps[j], gv[:, gcol + j:gcol + j + 1], rstd[:], ALU.mult, ALU.mult)
            kb.dma("sp", [ob], [], mixT[row0 + j * 128: row0 + (j + 1) * 128, cols], ob[:, j, :])

    for l in range(DEPTH):
        vv = vecs[l]
        xin = xT if l == 0 else xres
        with contextlib.ExitStack() as st:
            WIB = [0, 1024, NCOL]
            WiT = [kb.sb(st, [128, 8, WIB[i + 1] - WIB[i]], BF16, "Wi") for i in range(2)]
            for i in range(2):
                for kc in range(8):
                    kb.dma("pool", [], [WiT[i]], WiT[i][:, kc, :], w_in[l, kc * 128:(kc + 1) * 128, WIB[i]:WIB[i + 1]], part=(kc % 8 != 0))

            def wi(kc, c0, m):
                i = min(c0 // 1024, 1)
                return WiT[i], WiT[i][:, kc, c0 - WIB[i]:c0 - WIB[i] + m]
            Wuq = kb.sb(st, [128, 2, 512], BF16, "Wuq")
            for kc in range(2):
                kb.dma("pool", [], [Wuq], Wuq[:, kc, :], w_uq[l, kc * 128:(kc + 1) * 128, :])
            Wukv = kb.sb(st, [128, 512], BF16, "Wukv")
            kb.dma("pool", [], [Wukv], Wukv[:], w_ukv[l])
            rg = kb.sb(st, [128, 2, L], F32, "ropeg")
            rm = kb.sb(st, [128, 2, L], F32, "ropem")
            for i in range(2):
                kb.dma("sp", [], [rg], rg[:, i, :], ropeg_d[i])
                kb.dma("sp", [], [rm], rm[:, i, :], ropem_d[i])
            xb = [kb.sb(st, [128, 8, TB], F32, "xb") for _ in range(2)]
            sqb = kb.sb(st, [128, 8, TB], BF16, "sqb")
            hn = kb.sb(st, [128, 8, TB], BF16, "hn")
            scr = kb.sb(st, [128, TB], F32, "scr")
            rstd = kb.sb(st, [128, TB], F32, "rstd")
            stg = [kb.sb(st, [128, TB], BF16, "stg") for _ in range(5)]
            f1 = [kb.sb(st, [128, TB], F32, "f1") for _ in range(5)]
            ql = kb.sb(st, [128, 2, TB], F32, "ql")
            qn = kb.sb(st, [128, 2, TB], BF16, "qn")
            kvn = kb.sb(st, [128, TB], BF16, "kvn")
            sq2 = kb.sb(st, [128, 2, TB], BF16, "sq2")
            sqK = kb.sb(st, [128, TB], BF16, "sqK")
            sqG = kb.sb(st, [128, TB], BF16, "sqG")
            vst = [kb.sb(st, [128, 256], BF16, "vst") for _ in range(2)]
            si = {"s": 0, "f": 0, "v": 0}

            def nstg():
                si["s"] += 1
                return stg[si["s"] % 5]

            def nf1():
                si["f"] += 1
                return f1[si["f"] % 5]

            def load_x(tb):
                kb.dma("sp", [], [xb[tb % 2]], xb[tb % 2][:], xin.rearrange("(kc p) t -> p kc t", p=128)[:, :, tb * TB:(tb + 1) * TB])

            load_x(0)
            for tb in range(NB):
                cs = slice(tb * TB, (tb + 1) * TB)
                x_ = xb[tb % 2]
                if tb + 1 < NB:
                    load_x(tb + 1)
                kb.act([x_], [sqb], sqb[:], x_[:], AF.Square)
                rms_rstd(None, [(sqb, sqb[:, kc, :]) for kc in range(8)], 8, rstd, scr, eps_ap, 1.0 / D)
                for kc in range(8):
                    kb.stt([x_, vv, rstd], [hn], hn[:, kc, :], x_[:, kc, :], vv[:, V_ATTN + kc:V_ATTN + kc + 1], rstd[:],
                           ALU.mult, ALU.mult)

                def proj(c0, m=128):
                    p = nps()
                    kb.mmg(p, p[0:m, :], [(wi(kc, c0, m)[1], hn[:, kc, :]) for kc in range(8)], [wi(0, c0, m)[0], hn])
                    return p

                for j in range(2):
                    p = proj(768 + j * 128)
                    kb.act([p], [ql], ql[:, j, :], p[:], AF.Copy)
                    kb.act([p], [sq2], sq2[:, j, :], p[:], AF.Square)
                p = proj(1024)
                fk = nf1()
                kb.act([p], [fk], fk[:], p[:], AF.Copy)
                kb.act([p], [sqK], sqK[:], p[:], AF.Square)
                for j in range(2):
                    p = proj(j * 128)
                    s_ = nstg()
                    kb.act([p], [s_], s_[:], p[:], AF.Copy)
                    kb.dma("sp", [s_], [], uT[j * 128:(j + 1) * 128, cs], s_[:])
                for j in range(2):
                    pv = proj(256 + j * 128)
                    pg = proj(512 + j * 128)
                    f_ = nf1()
                    kb.act([pg], [f_], f_[:], pg[:], AF.Sigmoid)
                    s_ = nstg()
                    kb.tt([pv, f_], [s_], s_[:], pv[:], f_[:], ALU.mult)
                    kb.dma("sp", [s_], [], gT[j * 128:(j + 1) * 128, cs], s_[:])
                rq = nf1()
                rms_rstd(None, [(sq2, sq2[:, 0, :]), (sq2, sq2[:, 1, :])], 2, rq, scr, eps_ap, 1.0 / 256.0)
                for j in range(2):
                    kb.stt([ql, vv, rq], [qn], qn[:, j, :], ql[:, j, :], vv[:, V_MLAQ + j:V_MLAQ + j + 1], rq[:], ALU.mult, ALU.mult)
                rk = nf1()
                rms_rstd(None, [(sqK, sqK[:])], 1, rk, scr, eps_ap, 1.0 / 128.0)
                kb.stt([fk, vv, rk], [kvn], kvn[:], fk[:], vv[:, V_MLAKV:V_MLAKV + 1], rk[:], ALU.mult, ALU.mult)
                def gqa_part(c_main, c_swap, vg_col, vgs_col, bias_ap, scale, dst, heads):
                    p = proj(c_main)
                    p2 = proj(c_swap)
                    qa = nf1()
                    kb.act([p, vv], [qa], qa[:], p[:], AF.Identity, scale=vv[:, vg_col:vg_col + 1])
                    kb.act([p], [sqG], sqG[:], p[:], AF.Square)
                    qs = nf1()
                    kb.act([p2, vv], [qs], qs[:], p2[:], AF.Identity, scale=vv[:, vgs_col:vgs_col + 1])
                    rr = nf1()
                    rms_rstd(None, [(sqG, sqG[:])], 1, rr, scr, bias_ap, scale, ones=bones)
                    kb.tt([qa, rg], [qa], qa[:], qa[:], rg[:, 0, cs], ALU.mult)
                    kb.tt([qs, rg], [qs], qs[:], qs[:], rg[:, 1, cs], ALU.mult)
                    kb.tt([qa, qs], [qa], qa[:], qa[:], qs[:], ALU.add)
                    s_ = nstg()
                    kb.tt([qa, rr], [s_], s_[:], qa[:], rr[:], ALU.mult)
                    for hh in range(2):
                        kb.dma("sp", [s_], [], dst[heads[hh], :, cs], s_[hh * 64:(hh + 1) * 64, :])

                for j in range(2):
                    gqa_part(1152 + j * 128, 1408 + j * 128, V_GQ, V_GQS, eps64_ap, 1.0, qg, (2 * j, 2 * j + 1))
                gqa_part(1664, 1792, V_GK, V_GKS, eps_ap, 1.0 / 64.0, kg, (0, 1))
                sc_m = 96.0 ** -0.5
                for j in range(2):
                    p = nps()
                    kb.mmg(p, p[:], [(Wuq[:, kc, j * 128:(j + 1) * 128], qn[:, kc, :]) for kc in range(2)], [Wuq, qn])
                    s_ = nstg()
                    kb.act([p], [s_], s_[:], p[:], AF.Copy, scale=sc_m)
                    for hh in range(2):
                        kb.dma("sp", [s_], [], qm[2 * j + hh, 0:64, cs], s_[hh * 64:(hh + 1) * 64, :])
                pc = nps()
                kb.mmg(pc, pc[:], [(Wuq[:, kc, 256:384], qn[:, kc, :]) for kc in range(2)], [Wuq, qn])
                pd = nps()
                kb.mmg(pd, pd[:], [(Wuq[:, kc, 384:512], qn[:, kc, :]) for kc in range(2)], [Wuq, qn])
                fa = nf1()
                kb.tt([pc, rm], [fa], fa[:], pc[:], rm[:, 0, cs], ALU.mult)
                fb = nf1()
                kb.tt([pd, rm], [fb], fb[:], pd[:], rm[:, 1, cs], ALU.mult)
                kb.tt([fa, fb], [fa], fa[:], fa[:], fb[:], ALU.add)
                s_ = nstg()
                kb.act([fa], [s_], s_[:], fa[:], AF.Copy, scale=sc_m)
                for h in range(4):
                    kb.dma("sp", [s_], [], qm[h, 64:96, cs], s_[h * 32:(h + 1) * 32, :])
                for j in range(2):
                    p = nps()
                    kb.mmg(p, p[:], [(Wukv[:, j * 128:(j + 1) * 128], kvn[:])], [Wukv, kvn])
                    s_ = nstg()
                    kb.act([p], [s_], s_[:], p[:], AF.Copy)
                    for hh in range(2):
                        kb.dma("sp", [s_], [], km[2 * j + hh, 0:64, cs], s_[hh * 64:(hh + 1) * 64, :])
                for sbk in range(4):
                    p = nps()
                    kb.mmg(p, p[:, 0:256], [(kvn[:, sbk * 128:(sbk + 1) * 128], Wukv[:, 256:512])], [Wukv, kvn])
                    v_ = vst[sbk % 2]
                    kb.act([p], [v_], v_[:], p[:, 0:256], AF.Copy)
                    kb.dma("sp", [v_], [], vm[tb * TB + sbk * 128: tb * TB + (sbk + 1) * 128, :], v_[:])
                pa = proj(2048, 32)
                pb = proj(2080, 32)
                fa = nf1()
                kb.tt([pa, rm], [fa], fa[0:32, :], pa[0:32, :], rm[0:32, 0, cs], ALU.mult)
                fb = nf1()
                kb.tt([pb, rm], [fb], fb[0:32, :], pb[0:32, :], rm[0:32, 1, cs], ALU.mult)
                s_ = nstg()
                kb.tt([fa, fb], [s_], s_[0:32, :], fa[0:32, :], fb[0:32, :], ALU.add)
                for h in range(4):
                    kb.dma("sp", [s_], [], km[h, 64:96, cs], s_[0:32, :])
                for sbk in range(4):
                    p = nps()
                    kb.mmg(p, p[:, 0:128], [(hn[:, kc, sbk * 128:(sbk + 1) * 128], wi(kc, 1920, 128)[1]) for kc in range(8)], [WiT[1], hn])
                    v_ = vst[sbk % 2]
                    kb.act([p], [v_], v_[:, 0:128], p[:, 0:128], AF.Copy)
                    kb.dma("sp", [v_], [], vg[tb * TB + sbk * 128: tb * TB + (sbk + 1) * 128, :], v_[:, 0:128])
            kb.phase_end()
        if stop_after == ("P1", l):
            break
        with contextlib.ExitStack() as st:
            emit_s5h(kb, nc, st, l, vv, cst, jrow, s5s_d, s5b_d, cst2_d, w_glu, uT, psp, nps, group_norm_store, onesb)
            kb.phase_end()
        if stop_after == ("P2", l):
            break
        with contextlib.ExitStack() as st:
            dg = kb.sb(st, [128, 62, 128], BF16, "dg")
            for i in range(62):
                kb.ts([cst, vv], [dg], dg[:, i, :], ident, vv[:, V_CVW + i:V_CVW + i + 1], ALU.mult)
            gb = [kb.sb(st, [128, 2, TB + 30], BF16, "gb") for _ in range(2)]
            def load_g(tb):
                g_ = gb[tb % 2]
                lo = tb * TB - 15
                hi = tb * TB + TB + 15
                a, b = max(lo, 0), min(hi, L)
                if lo < 0 or hi > L:
                    kb.memset([g_], g_[:], 0.0)
                kb.dma("sp", [], [g_], g_[:, :, a - lo: b - lo], gT.rearrange("(m p) t -> p m t", p=128)[:, :, a:b])

            sets = []
            for _ in range(2):
                d_ = {}
                for nm in ("cv", "cq", "xn", "sgm"):
                    d_[nm] = kb.sb(st, [128, 2, TB], F32, nm)
                for nm in ("mean", "var", "scr", "rstd"):
                    d_[nm] = kb.sb(st, [128, TB], F32, nm)
                d_["sq"] = kb.sb(st, [128, 2, TB], BF16, "sq")
                d_["ob"] = kb.sb(st, [128, 2, TB], BF16, "ob")
                sets.append(d_)
            for tb in range(NB):
                d_ = sets[tb % 2]
                cv, cq, xn, sgm = d_["cv"], d_["cq"], d_["xn"], d_["sgm"]
                mean, var, scr, rstd, sq, ob = d_["mean"], d_["var"], d_["scr"], d_["rstd"], d_["sq"], d_["ob"]
                g_ = gb[tb % 2]
                if tb == 0:
                    load_g(0)
                if tb + 1 < NB:
                    load_g(tb + 1)
                for m in range(2):
                    p = nps(5)
                    kb.mmg(p, p[:], [(dg[:, m * 31 + k, :], g_[:, m, k:k + TB]) for k in range(31)], [dg, g_])
                    kb.act([p, vv], [cv], cv[:, m, :], p[:], AF.Identity, bias=vv[:, V_CVB + m:V_CVB + m + 1])
                    kb.act([cv], [cq], cq[:, m, :], cv[:, m, :], AF.Square)
                s1 = psp[6] if tb % 2 == 0 else psp[5]
                kb.mmg(s1, s1[:], [(onesf[:], cv[:, m, :]) for m in range(2)], [onesf, cv])
                s2 = psp[7]
                kb.mmg(s2, s2[:], [(onesf[:], cq[:, m, :]) for m in range(2)], [onesf, cq])
                kb.act([s1], [mean], mean[:], s1[:], AF.Copy, scale=1.0 / 256.0)
                kb.tt([mean], [var], var[:], mean[:], mean[:], ALU.mult)
                kb.stt([s2, var], [var], var[:], s2[:], 1.0 / 256.0, var[:], ALU.mult, ALU.subtract)
                kb.act([var, epsb], [scr], scr[:], var[:], AF.Sqrt, bias=eps_ap)
                kb.recip([scr], [rstd], rstd[:], scr[:])
                for m in range(2):
                    kb.tt([cv, mean], [cv], cv[:, m, :], cv[:, m, :], mean[:], ALU.subtract)
                    kb.tt([cv, rstd], [cv], cv[:, m, :], cv[:, m, :], rstd[:], ALU.mult)
                    kb.act([cv, vv], [xn], xn[:, m, :], cv[:, m, :], AF.Identity, scale=vv[:, V_LNG + m:V_LNG + m + 1],
                           bias=vv[:, V_LNB + m:V_LNB + m + 1])
                    kb.act([cv, vv], [sgm], sgm[:, m, :], cv[:, m, :], AF.Sigmoid, scale=vv[:, V_LNG + m:V_LNG + m + 1],
                           bias=vv[:, V_LNB + m:V_LNB + m + 1])
                    kb.tt([xn, sgm], [xn], xn[:, m, :], xn[:, m, :], sgm[:, m, :], ALU.mult)
                group_norm_store(st, xn, [xn[:, 0, :], xn[:, 1, :]], vv, V_MIX + 2, 256, slice(tb * TB, (tb + 1) * TB), sq, scr, rstd, ob)
            kb.phase_end()
        if stop_after == ("P3", l):
            break
        with contextlib.ExitStack() as st:
            qt = [[kb.sb(st, [128, L], BF16, "qt") for _ in range(2)] for _ in range(2)]
            kt = [[kb.sb(st, [128, L], BF16, "kt") for _ in range(2)] for _ in range(2)]
            va = [[kb.sb(st, [128, 32, 128], BF16, "va") for _ in range(2)] for _ in range(2)]
            for s_ in range(2):
                for i in range(2):
                    kb.memset([va[s_][i]], va[s_][i][:], 1.0, e="pool" if i else "dve")
            yg = [kb.sb(st, [128, L], F32, "yg") for _ in range(2)]
            pt = [kb.sb(st, [128, 2 * TB], BF16, "pt") for _ in range(3)]
            rec = [kb.sb(st, [128, TB], F32, "rec") for _ in range(2)]
            scr = kb.sb(st, [128, TB], F32, "scr")
            rstd = kb.sb(st, [128, TB], F32, "rstd")
            sq = kb.sb(st, [128, 2, TB], BF16, "sq")
            ob = kb.sb(st, [128, 2, TB], BF16, "ob")
            mx = [kb.sb(st, [128, 2, 2, 10], F32, "mx") for _ in range(2)]
            negb = [kb.sb(st, [128, 2], F32, "negb") for _ in range(2)]
            S2 = [Tl(pbig[0].h), Tl(pbig[1].h)]
            Oh = [psp[4], psp[5]]
            pairs = [(0, 0), (0, 1), (1, 0), (1, 1)]

            sqq4 = [kb.sb(st, [96, L], BF16, "sqq4") for _ in range(4)]

            def setup_load(pidx):
                grp, pr = pairs[pidx]
                s_ = pidx % 2
                for i in range(2):
                    h = 2 * pr + i
                    q_, k_, v_ = qt[s_][i], kt[s_][i], va[s_][i]
                    if grp == 1:
                        kb.memset([q_], q_[64:128, :], 0.0, e="pool")
                        kb.memset([k_], k_[64:128, :], 0.0, e="pool")
                        kb.dma("sp", [], [q_], q_[0:64, :], qg[h])
                        kb.dma("sp", [], [k_], k_[0:64, :], kg[pr])
                        kb.dma("sp", [], [v_], v_[:, :, i * 64:(i + 1) * 64],
                               vg.rearrange("(kb p) c -> p kb c", p=128)[:, :, pr * 64:(pr + 1) * 64])
                    else:
                        kb.dma("sp", [], [q_], q_[0:96, :], qm[h])
                        kb.dma("sp", [], [k_], k_[0:96, :], km[h])
                        kb.dma("sp", [], [v_], v_[:, :, i * 64:(i + 1) * 64],
                               vm.rearrange("(kb p) c -> p kb c", p=128)[:, :, h * 64:(h + 1) * 64])

            def setup_sq(pidx, only=None):
                grp, pr = pairs[pidx]
                s_ = pidx % 2
                dk = 96 if grp == 0 else 64
                for i in range(2):
                    for w_, src in ((0, qt[s_][i]), (1, kt[s_][i])):
                        if only is not None and only != 2 * i + w_:
                            continue
                        kb.act([src], [sqq4[2 * i + w_]], sqq4[2 * i + w_][0:dk, :], src[0:dk, :], AF.Square)

            def setup_bound(pidx):
                grp, pr = pairs[pidx]
                s_ = pidx % 2
                dk = 96 if grp == 0 else 64
                mx_ = mx[s_]
                for i in range(2):
                    for w_ in range(2):
                        sq_ = sqq4[2 * i + w_]
                        for blk in range(NB):
                            pss_ = psp[6 + (blk % 2)]
                            kb.mmg(pss_, pss_[:], [(onesb[0:dk, :], sq_[0:dk, blk * TB:(blk + 1) * TB])], [onesb, sq_])
                            kb.op("dve", [pss_], [mx_], lambda: nc.vector.reduce_max(
                                out=mx_[:, i, w_, blk:blk + 1], in_=pss_[:], axis=mybir.AxisListType.X))
                        kb.op("dve", [mx_], [mx_], lambda: nc.vector.reduce_max(
                            out=mx_[:, i, w_, 8:9], in_=mx_[:, i, w_, 0:8], axis=mybir.AxisListType.X))
                    kb.tt([mx_], [mx_], mx_[:, i, 0, 9:10], mx_[:, i, 0, 8:9], mx_[:, i, 1, 8:9], ALU.mult)
                    kb.act([mx_], [mx_], mx_[:, i, 1, 9:10], mx_[:, i, 0, 9:10], AF.Sqrt)
                    kb.ts([mx_], [negb[s_]], negb[s_][:, i:i + 1], mx_[:, i, 1, 9:10], -1.0, ALU.mult)

            def setup(pidx):
                setup_load(pidx)
                setup_sq(pidx)
                setup_bound(pidx)

            def gnorm_block(grp, tb):
                cs = slice(tb * TB, (tb + 1) * TB)
                for j in range(2):
                    kb.act([yg[j]], [sq], sq[:, j, :], yg[j][:, cs], AF.Square)
                rms_rstd(None, [(sq, sq[:, 0, :]), (sq, sq[:, 1, :])], 2, rstd, scr, eps_ap, 1.0 / 256.0)
                gc = V_MIX + 4 + 2 * grp
                for j in range(2):
                    kb.stt([yg[j], vv, rstd], [ob], ob[:, j, :], yg[j][:, cs], vv[:, gc + j:gc + j + 1], rstd[:], ALU.mult, ALU.mult)
                    r0 = 512 + 256 * grp + j * 128
                    kb.dma("sp", [ob], [], mixT[r0:r0 + 128, cs], ob[:, j, :])

            setup(0)
            deferred = []
            pi = 0
            for pidx in range(4):
                grp, pr = pairs[pidx]
                s_ = pidx % 2
                dkm = 96 if grp == 0 else 128
                pending = []
                for qb in range(NB):
                    qs_ = slice(qb * TB, (qb + 1) * TB)
                    if deferred:
                        gnorm_block(*deferred.pop(0))
                    if pidx + 1 < 4:
                        if qb == 1:
                            setup_load(pidx + 1)
                        elif 2 <= qb <= 5:
                            setup_sq(pidx + 1, only=qb - 2)
                        elif qb == 6:
                            setup_bound(pidx + 1)
                    for i in range(2):
                        O = Oh[i]
                        q_, k_, v_ = qt[s_][i], kt[s_][i], va[s_][i]
                        for kb2 in range(16):
                            S = S2[pi % 2]
                            P = pt[pi % 3]
                            pi += 1
                            for hf in range(2):
                                kbk = 2 * kb2 + hf
                                kb.mm_acc(S, S[:, hf * TB:(hf + 1) * TB], k_[0:dkm, kbk * 128:(kbk + 1) * 128], q_[0:dkm, qs_],
                                          [k_, q_], True, True, hf == 0)
                            kb.act([S, negb[s_]], [P], P[:], S[:], AF.Exp, bias=negb[s_][:, i:i + 1])

                            def fin(O=O, P=P, i=i, kb2=kb2, qs_=qs_, pr=pr, v_=v_):
                                for hf in range(2):
                                    kbk = 2 * kb2 + hf
                                    first = (kb2 == 0 and hf == 0)
                                    last = (kb2 == 15 and hf == 1)
                                    kb.mm_acc(O, O[:], v_[:, kbk, :], P[:, hf * TB:(hf + 1) * TB], [v_, P], first, last, first)
                                if kb2 == 15:
                                    a0, d0 = (0, 64) if i == 0 else (64, 0)
                                    kb.recip([O], [rec[i]], rec[i][a0:a0 + 64, :], O[d0:d0 + 64, :])
                                    kb.tt([O, rec[i]], [yg[pr]], yg[pr][a0:a0 + 64, qs_], O[a0:a0 + 64, :], rec[i][a0:a0 + 64, :], ALU.mult)

                            pending.append(fin)
                            if len(pending) > 1:
                                pending.pop(0)()
                while pending:
                    pending.pop(0)()
                if pr == 1:
                    deferred += [(grp, tb) for tb in range(NB)]
            while deferred:
                gnorm_block(*deferred.pop(0))
            kb.phase_end()
        if stop_after == ("P4", l):
            break
        with contextlib.ExitStack() as st:
            Wo = [kb.sb(st, [128, 8, 512], BF16, "Wo") for _ in range(2)]
            for cg in range(2):
                for kc in range(8):
                    kb.dma("pool", [], [Wo[cg]], Wo[cg][:, kc, :], w_out[l, kc * 128:(kc + 1) * 128, cg * 512:(cg + 1) * 512], part=(kc % 8 != 0))
            mb = [kb.sb(st, [128, 8, TB], BF16, "mb") for _ in range(2)]
            xb = [kb.sb(st, [128, 8, TB], F32, "xb") for _ in range(2)]
            x1 = [kb.sb(st, [128, 8, TB], F32, "x1") for _ in range(2)]
            sqb = kb.sb(st, [128, 8, TB], BF16, "sqb")
            hb = [kb.sb(st, [128, 8, TB], BF16, "hb") for _ in range(2)]
            scr = kb.sb(st, [128, TB], F32, "scr")
            rstd = kb.sb(st, [128, TB], F32, "rstd")
            def load_a(tb):
                cs = slice(tb * TB, (tb + 1) * TB)
                kb.dma("sp", [], [mb[tb % 2]], mb[tb % 2][:], mixT.rearrange("(kc p) t -> p kc t", p=128)[:, :, cs])
                kb.dma("sp", [], [xb[tb % 2]], xb[tb % 2][:], xin.rearrange("(kc p) t -> p kc t", p=128)[:, :, cs])

            load_a(0)
            for tb in range(NB):
                cs = slice(tb * TB, (tb + 1) * TB)
                m_, x_, x1_, h_ = mb[tb % 2], xb[tb % 2], x1[tb % 2], hb[tb % 2]
                if tb + 1 < NB:
                    load_a(tb + 1)
                for oc in range(8):
                    p = nps()
                    kb.mmg(p, p[:], [(Wo[oc // 4][:, kc, (oc % 4) * 128:(oc % 4 + 1) * 128], m_[:, kc, :]) for kc in range(8)], [Wo[oc // 4], m_])
                    kb.tt([p, x_], [x1_], x1_[:, oc, :], p[:], x_[:, oc, :], ALU.add)
                kb.dma("act", [x1_], [], x1T.rearrange("(kc p) t -> p kc t", p=128)[:, :, cs], x1_[:])
                kb.act([x1_], [sqb], sqb[:], x1_[:], AF.Square)
                rms_rstd(None, [(sqb, sqb[:, kc, :]) for kc in range(8)], 8, rstd, scr, eps_ap, 1.0 / D)
                for kc in range(8):
                    kb.stt([x1_, vv, rstd], [h_], h_[:, kc, :], x1_[:, kc, :], vv[:, V_FFN + kc:V_FFN + kc + 1], rstd[:], ALU.mult, ALU.mult)
                kb.dma("act", [h_], [], hn2T.rearrange("(kc p) t -> p kc t", p=128)[:, :, cs], h_[:])
            kb.phase_end()
        wd_scope = contextlib.ExitStack()
        Wd = [kb.sb(wd_scope, [128, 22, 512], BF16, "Wd") for _ in range(2)]
        with contextlib.ExitStack() as st:
            WuG = []
            for g in range(3):
                w = min(8, 22 - 8 * g) * 128
                t_ = kb.sb(st, [128, 8, 2 * w], BF16, "WuG")
                for kc in range(8):
                    kb.dma("pool", [], [t_], t_[:, kc, 0:w], w_up[l, kc * 128:(kc + 1) * 128, g * 1024:g * 1024 + w], part=(kc % 8 != 0))
                    kb.dma("pool", [], [t_], t_[:, kc, w:2 * w], w_up[l, kc * 128:(kc + 1) * 128, DFF + g * 1024:DFF + g * 1024 + w], part=True)
                WuG.append((t_, w))
            for cg in range(2):
                for kc in range(22):
                    kb.dma("pool", [], [Wd[cg]], Wd[cg][:, kc, :], w_down[l, kc * 128:(kc + 1) * 128, cg * 512:(cg + 1) * 512], part=(kc % 8 != 0))
            hbk = [kb.sb(st, [128, 8, TB], BF16, "hbk") for _ in range(2)]
            ag = [kb.sb(st, [128, TB], F32, "ag") for _ in range(2)]
            au = [kb.sb(st, [128, TB], F32, "au") for _ in range(2)]
            sg_ = [kb.sb(st, [128, TB], F32, "sg") for _ in range(2)]
            ab = [kb.sb(st, [128, TB], BF16, "ab") for _ in range(3)]
            SV = [kb.sb(st, [128, 44, 2], F32, "SV") for _ in range(2)]
            kb.memset([SV[0]], SV[0][:], 0.0)
            it = 0
            for tb in range(NB):
                h_ = hbk[tb % 2]
                kb.dma("sp", [], [h_], h_[:], hn2T.rearrange("(kc p) t -> p kc t", p=128)[:, :, tb * TB:(tb + 1) * TB])
                svp, svn = SV[tb % 2], SV[(tb + 1) % 2]
                for j in range(22):
                    wt, w = WuG[j // 8]
                    accs = []
                    for half in range(2):
                        jj = half * 22 + j
                        c0 = half * w + (j % 8) * 128
                        p = nps(8)
                        kb.mmg(p, p[:], [(wt[:, kc, c0:c0 + 128], h_[:, kc, :]) for kc in range(8)], [wt, h_])
                        acc = (ag if half == 0 else au)[(it // 2) % 2]
                        wb = V_FCW + jj * 3
                        kb.act([p, vv], [acc], acc[:], p[:], AF.Identity, scale=vv[:, wb + 2:wb + 3], bias=vv[:, V_FCB + jj:V_FCB + jj + 1])
                        kb.stt([p, vv, acc], [acc], acc[:, 1:TB], p[:, 0:TB - 1], vv[:, wb + 1:wb + 2], acc[:, 1:TB], ALU.mult, ALU.add)
                        kb.stt([p, vv, acc], [acc], acc[:, 2:TB], p[:, 0:TB - 2], vv[:, wb:wb + 1], acc[:, 2:TB], ALU.mult, ALU.add)
                        kb.stt([svp, vv, acc], [acc], acc[:, 0:2], svp[:, jj, 0:2], vv[:, wb:wb + 1], acc[:, 0:2], ALU.mult, ALU.add)
                        kb.stt([svp, vv, acc], [acc], acc[:, 0:1], svp[:, jj, 1:2], vv[:, wb + 1:wb + 2], acc[:, 0:1], ALU.mult, ALU.add)
                        kb.act([p], [svn], svn[:, jj, :], p[:, TB - 2:TB], AF.Copy)
                        accs.append(acc)
                        it += 1
                    g_, u_ = accs
                    s_ = sg_[j % 2]
                    kb.act([g_], [s_], s_[:], g_[:], AF.Silu)
                    a_ = ab[j % 3]
                    kb.tt([s_, u_], [a_], a_[:], s_[:], u_[:], ALU.mult, e="pool")
                    if tb == 0:
                        kb.dma("sp", [a_], [], actT[j * 128:(j + 1) * 128, 0:TB - 1], a_[:, 1:TB])
                    else:
                        kb.dma("sp", [a_], [], actT[j * 128:(j + 1) * 128, tb * TB - 1:tb * TB + TB - 1], a_[:])
            svl = SV[NB % 2]
            tl = kb.sb(st, [128, 44], F32, "tl")
            tl2 = kb.sb(st, [128, 44], F32, "tl2")
            tlb = kb.sb(st, [128, 22], BF16, "tlb")
            kb.tt([svl, vv], [tl], tl[:], svl[:, :, 1], vv[:, V_FCW + 1:V_FCW + 132:3], ALU.mult)
            kb.tt([svl, vv], [tl2], tl2[:], svl[:, :, 0], vv[:, V_FCW:V_FCW + 132:3], ALU.mult)
            kb.tt([tl, tl2], [tl], tl[:], tl[:], tl2[:], ALU.add)
            kb.tt([tl, vv], [tl], tl[:], tl[:], vv[:, V_FCB:V_FCB + 44], ALU.add)
            kb.act([tl], [tl2], tl2[:, 0:22], tl[:, 0:22], AF.Silu)
            kb.tt([tl2, tl], [tlb], tlb[:], tl2[:, 0:22], tl[:, 22:44], ALU.mult)
            with nc.allow_non_contiguous_dma(reason="single-column tail store"):
                kb.dma("sp", [tlb], [], actT.rearrange("(j p) t -> p j t", p=128)[:, :, L - 1:L], tlb[:].unsqueeze(2))
            kb.phase_end()
        with contextlib.ExitStack() as st:

            abk = [kb.sb(st, [128, 22, TB], BF16, "abk") for _ in range(2)]
            x1 = [kb.sb(st, [128, 8, TB], F32, "x1") for _ in range(2)]
            x2 = [kb.sb(st, [128, 8, TB], F32, "x2") for _ in range(1)]
            sqb = kb.sb(st, [128, 8, TB], BF16, "sqb")
            scr = kb.sb(st, [128, TB], F32, "scr")
            rstd = kb.sb(st, [128, TB], F32, "rstd")
            last = (l == DEPTH - 1)
            def load_c(tb):
                cs = slice(tb * TB, (tb + 1) * TB)
                kb.dma("sp", [], [abk[tb % 2]], abk[tb % 2][:], actT.rearrange("(kc p) t -> p kc t", p=128)[:, :, cs])
                kb.dma("sp", [], [x1[tb % 2]], x1[tb % 2][:], x1T.rearrange("(kc p) t -> p kc t", p=128)[:, :, cs])

            load_c(0)
            for tb in range(NB):
                cs = slice(tb * TB, (tb + 1) * TB)
                a_, x1_, x2_ = abk[tb % 2], x1[tb % 2], x2[0]
                if tb + 1 < NB:
                    load_c(tb + 1)
                for oc in range(8):
                    p = nps()
                    kb.mmg(p, p[:], [(Wd[oc // 4][:, kc, (oc % 4) * 128:(oc % 4 + 1) * 128], a_[:, kc, :]) for kc in range(22)], [Wd[oc // 4], a_])
                    kb.tt([p, x1_], [x2_], x2_[:, oc, :], p[:], x1_[:, oc, :], ALU.add)
                if not last:
                    kb.dma("act", [x2_], [], xres.rearrange("(kc p) t -> p kc t", p=128)[:, :, cs], x2_[:])
                else:
                    kb.act([x2_], [sqb], sqb[:], x2_[:], AF.Square)
                    rms_rstd(None, [(sqb, sqb[:, kc, :]) for kc in range(8)], 8, rstd, scr, eps_ap, 1.0 / D)
                    for kc in range(8):
                        kb.stt([x2_, vv, rstd], [x1_], x1_[:, kc, :], x2_[:, kc, :], vv[:, V_FIN + kc:V_FIN + kc + 1], rstd[:], ALU.mult, ALU.mult)
                    kb.dma("act", [x1_], [], outT.rearrange("(kc p) t -> p kc t", p=128)[:, :, cs], x1_[:])
            kb.phase_end()
        wd_scope.close()
        if stop_after == ("P5", l):
            break
    kb.phase_end()
    return nc


def sincos(kb, nc, arg_t, arg_ap, shape, out_s, out_s_ap, out_c, out_c_ap, k_t, r_t):
    C1 = 6.28125
    C2 = TWO_PI - 6.28125
    hp = kb.halfpi
    kb.ts([arg_t], [k_t], k_t[:], arg_ap, 1.0 / TWO_PI, ALU.mult, MAGIC, ALU.add)
    kb.ts([k_t], [k_t], k_t[:], k_t[:], -MAGIC, ALU.add)
    kb.stt([k_t, arg_t], [r_t], r_t[:], k_t[:], -C1, arg_ap, ALU.mult, ALU.add)
    kb.stt([k_t, r_t], [r_t], r_t[:], k_t[:], -C2, r_t[:], ALU.mult, ALU.add)
    kb.ts([r_t], [r_t], r_t[:], r_t[:], 3.14159, ALU.min, -3.14159, ALU.max)
    kb.act([r_t], [out_s], out_s_ap, r_t[:], AF.Sin)
    kb.act([r_t], [k_t], k_t[:], r_t[:], AF.Sin, scale=0.5)
    kb.tt([k_t], [k_t], k_t[:], k_t[:], k_t[:], ALU.mult)
    kb.ts([k_t], [out_c], out_c_ap, k_t[:], -2.0, ALU.mult, 1.0, ALU.add)


def emit_s5(kb, nc, st, l, vv, cst, jrow, s5c_d, s5s_d, wc_d, w_glu, uT, psp, nps, group_norm_store, eps_ap, rms_rstd):
    st_outer = st
    Wg = kb.sb(st_outer, [128, 2, 256], BF16, "Wg")
    for kc in range(2):
        kb.dma("pool", [], [Wg], Wg[:, kc, :], w_glu[l, kc * 128:(kc + 1) * 128, :])
    ub = kb.sb(st_outer, [128, 2, L], BF16, "ub")
    kb.dma("sp", [], [ub], ub[:], uT.rearrange("(m p) t -> p m t", p=128))
    Y = kb.sb(st_outer, [128, 2, L], F32, "Y")
    st = contextlib.ExitStack()
    pc = kb.sb(st, [128, 5, 256], F32, "s5c")
    kb.dma("sp", [], [pc], pc[:], s5c_d[l].rearrange("p (a b) -> p a b", a=5))
    pss = kb.sb(st, [128, 3, 16], F32, "s5s")
    kb.dma("sp", [], [pss], pss[:], s5s_d[l].rearrange("p (a b) -> p a b", a=3))
    wcf = kb.sb(st, [128, 2, 16, 128], BF16, "wc")
    for ri in range(2):
        for hf in range(2):
            kb.dma("pool", [], [wcf], wcf[:, ri, hf * 8:(hf + 1) * 8, :],
                   wc_d[l].rearrange("p (a b c) -> p a b c", a=2, b=16)[:, ri, hf * 8:(hf + 1) * 8, :])
    kb.ts([wcf], [wcf], wcf[:, 1], wcf[:, 1], -1.0, ALU.mult)
    def tmp(n, nm):
        return kb.sb(st, [128, n], F32, nm)

    n = 256
    dl, are, aim, mag, sn, cs_, k_t, r_t = [tmp(n, x) for x in ("dl", "are", "aim", "mag", "sn", "cs", "kt", "rt")]
    kb.act([pc], [dl], dl[:], pc[:, 2, :], AF.Exp)
    kb.tt([pc, dl], [are], are[:], pc[:, 0, :], dl[:], ALU.mult)
    kb.tt([pc, dl], [aim], aim[:], pc[:, 1, :], dl[:], ALU.mult)
    kb.act([are], [mag], mag[:], are[:], AF.Exp)
    sincos(kb, nc, aim, aim[:], n, sn, sn[:], cs_, cs_[:], k_t, r_t)
    ar1, ai, den, fr, fi, t1, t2 = [tmp(n, x) for x in ("ar1", "ai", "den", "fr", "fi", "t1", "t2")]
    kb.tt([mag, cs_], [ar1], ar1[:], mag[:], cs_[:], ALU.mult)
    kb.ts([ar1], [ar1], ar1[:], ar1[:], -1.0, ALU.add)
    kb.tt([mag, sn], [ai], ai[:], mag[:], sn[:], ALU.mult)
    kb.tt([pc], [den], den[:], pc[:, 0, :], pc[:, 0, :], ALU.mult)
    kb.tt([pc], [t1], t1[:], pc[:, 1, :], pc[:, 1, :], ALU.mult)
    kb.tt([den, t1], [den], den[:], den[:], t1[:], ALU.add)
    kb.recip([den], [den], den[:], den[:])
    kb.tt([ar1, pc], [fr], fr[:], ar1[:], pc[:, 0, :], ALU.mult)
    kb.tt([ai, pc], [t1], t1[:], ai[:], pc[:, 1, :], ALU.mult)
    kb.tt([fr, t1], [fr], fr[:], fr[:], t1[:], ALU.add)
    kb.tt([fr, den], [fr], fr[:], fr[:], den[:], ALU.mult)
    kb.tt([ai, pc], [fi], fi[:], ai[:], pc[:, 0, :], ALU.mult)
    kb.tt([ar1, pc], [t1], t1[:], ar1[:], pc[:, 1, :], ALU.mult)
    kb.tt([fi, t1], [fi], fi[:], fi[:], t1[:], ALU.subtract)
    kb.tt([fi, den], [fi], fi[:], fi[:], den[:], ALU.mult)
    bbr, bbi = tmp(n, "bbr"), tmp(n, "bbi")
    kb.tt([fr, pc], [bbr], bbr[:], fr[:], pc[:, 3, :], ALU.mult)
    kb.tt([fi, pc], [t1], t1[:], fi[:], pc[:, 4, :], ALU.mult)
    kb.tt([bbr, t1], [bbr], bbr[:], bbr[:], t1[:], ALU.subtract)
    kb.tt([fr, pc], [bbi], bbi[:], fr[:], pc[:, 4, :], ALU.mult)
    kb.tt([fi, pc], [t1], t1[:], fi[:], pc[:, 3, :], ALU.mult)
    kb.tt([bbi, t1], [bbi], bbi[:], bbi[:], t1[:], ALU.add)
    WB = kb.sb(st, [128, 2, 16, 128], BF16, "WB")
    for ri, src in ((0, bbr), (1, bbi)):
        for d in range(2):
            for q in range(8):
                m = q // 4
                for gl in range(2):
                    o = (m * 2 + d) * 64
                    kb.ts([src, cst], [WB], WB[:, ri, d * 8 + q, gl * 64:(gl + 1) * 64], src[:, o:o + 64],
                          cst[:, 256 + q * 2 + gl:256 + q * 2 + gl + 1], ALU.mult)
    n2 = 16
    dl2, rdec, th, e_s, e_c, th512, k2, r2 = [tmp(n2, x) for x in ("dl2", "rdec", "th", "es", "ec", "th512", "k2", "r2")]
    kb.act([pss], [dl2], dl2[:], pss[:, 2, :], AF.Exp)
    kb.tt([pss, dl2], [rdec], rdec[:], pss[:, 0, :], dl2[:], ALU.mult)
    kb.act([rdec], [rdec], rdec[:], rdec[:], AF.Exp)
    kb.tt([pss, dl2], [th], th[:], pss[:, 1, :], dl2[:], ALU.mult)
    kb.ts([th], [th512], th512[:], th[:], 512.0, ALU.mult)
    sincos(kb, nc, th512, th512[:], n2, e_s, e_s[:], e_c, e_c[:], k2, r2)

    tc_ = kb.sb(st, [128, 8, TB], F32, "tcos")
    ts_ = kb.sb(st, [128, 8, TB], F32, "tsin")
    argt = kb.sb(st, [128, TB], F32, "argt")
    kk = kb.sb(st, [128, TB], F32, "kk")
    rr = kb.sb(st, [128, TB], F32, "rr")
    w1 = [kb.sb(st, [128, TB], F32, "w1") for _ in range(2)]
    w2 = [kb.sb(st, [128, TB], F32, "w2") for _ in range(2)]
    xr_ = [kb.sb(st, [128, TB], F32, "xpr") for _ in range(2)]
    xi_ = [kb.sb(st, [128, TB], F32, "xpi") for _ in range(2)]
    Gr = [kb.sb(st, [128, TB], F32, "Gr") for _ in range(2)]
    Gi = [kb.sb(st, [128, TB], F32, "Gi") for _ in range(2)]
    sr = [kb.sb(st, [128, TB], BF16, "sr") for _ in range(8)]
    si = [kb.sb(st, [128, TB], BF16, "si") for _ in range(8)]
    car = kb.sb(st, [128, 8, 4], F32, "car")
    it = 0
    for d in range(2):
        for q in range(8):
            kb.ts([cst, th], [argt], argt[:], jrow, th[:, d * 8 + q:d * 8 + q + 1], ALU.mult)
            sincos(kb, nc, argt, argt[:], TB, ts_, ts_[:, q, :], tc_, tc_[:, q, :], kk, rr)
        kb.memset([car], car[:], 0.0)
        for c in range(NB):
            tb = c if d == 0 else NB - 1 - c
            cs = slice(tb * TB, (tb + 1) * TB)

            def rv(ap):
                return ap if d == 0 else ap[:, ::-1]

            for q in range(8):
                m = q // 4
                dq = d * 8 + q
                pr_ = psp[(2 * it) % 4]
                pi_ = psp[(2 * it + 1) % 4]
                kb.mmg(pr_, pr_[:], [(WB[:, 0, dq, :], ub[:, m, cs])], [WB, ub])
                kb.mmg(pi_, pi_[:], [(WB[:, 1, dq, :], ub[:, m, cs])], [WB, ub])
                a1, a2 = w1[it % 2], w2[it % 2]
                xr, xi = xr_[it % 2], xi_[it % 2]
                gr, gi = Gr[it % 2], Gi[it % 2]
                C, S = tc_[:, q, :], ts_[:, q, :]
                kb.tt([pr_, tc_], [a1], a1[:], rv(pr_[:]), C, ALU.mult)
                kb.tt([pi_, ts_], [a2], a2[:], rv(pi_[:]), S, ALU.mult)
                kb.tt([a1, a2], [xr], xr[:], a1[:], a2[:], ALU.add)
                kb.tt([pi_, tc_], [a1], a1[:], rv(pi_[:]), C, ALU.mult)
                kb.tt([pr_, ts_], [a2], a2[:], rv(pr_[:]), S, ALU.mult)
                kb.tt([a1, a2], [xi], xi[:], a1[:], a2[:], ALU.subtract)
                dec = rdec[:, dq:dq + 1].to_broadcast([128, TB])
                kb.op("dve", [rdec, xr, car], [gr], lambda: nc.vector.tensor_tensor_scan(
                    out=gr[:], data0=dec, data1=xr[:], initial=car[:, q, 0:1], op0=ALU.mult, op1=ALU.add))
                kb.op("dve", [rdec, xi, car], [gi], lambda: nc.vector.tensor_tensor_scan(
                    out=gi[:], data0=dec, data1=xi[:], initial=car[:, q, 1:2], op0=ALU.mult, op1=ALU.add))
                kb.ts([gi, e_s], [car], car[:, q, 2:3], gi[:, TB - 1:TB], e_s[:, dq:dq + 1], ALU.mult)
                kb.ts([gi, e_c], [car], car[:, q, 3:4], gi[:, TB - 1:TB], e_c[:, dq:dq + 1], ALU.mult)
                kb.stt([gr, e_c, car], [car], car[:, q, 0:1], gr[:, TB - 1:TB], e_c[:, dq:dq + 1], car[:, q, 2:3], ALU.mult, ALU.subtract)
                kb.stt([gr, e_s, car], [car], car[:, q, 1:2], gr[:, TB - 1:TB], e_s[:, dq:dq + 1], car[:, q, 3:4], ALU.mult, ALU.add)
                s_r, s_i = sr[q], si[q]
                kb.tt([gr, tc_], [a1], a1[:], gr[:], C, ALU.mult)
                kb.tt([gi, ts_], [a2], a2[:], gi[:], S, ALU.mult)
                kb.tt([a1, a2], [s_r], rv(s_r[:]), a1[:], a2[:], ALU.subtract)
                kb.tt([gr, ts_], [a1], a1[:], gr[:], S, ALU.mult)
                kb.tt([gi, tc_], [a2], a2[:], gi[:], C, ALU.mult)
                kb.tt([a1, a2], [s_i], rv(s_i[:]), a1[:], a2[:], ALU.add)
                it += 1
                if q % 4 == 3:
                    py = psp[4 + m]
                    terms = []
                    rd = [wcf]
                    for q2 in range(4 * m, 4 * m + 4):
                        terms.append((wcf[:, 0, d * 8 + q2, :], sr[q2][:]))
                        terms.append((wcf[:, 1, d * 8 + q2, :], si[q2][:]))
                        rd += [sr[q2], si[q2]]
                    kb.mmg(py, py[:], terms, rd)
                    if d == 0:
                        kb.act([py], [Y], Y[:, m, cs], py[:], AF.Copy)
                    else:
                        kb.tt([py, Y], [Y], Y[:, m, cs], py[:], Y[:, m, cs], ALU.add)
    kb.phase_end()
    st.close()
    st = st_outer
    yb = kb.sb(st, [128, 2, TB], F32, "yb")
    wk = kb.sb(st, [128, 2, TB], F32, "wk")
    geb = kb.sb(st, [128, 2, TB], BF16, "geb")
    gef = kb.sb(st, [128, 2, TB], F32, "gef")
    sgl = kb.sb(st, [128, 2, TB], F32, "sgl")
    sq = kb.sb(st, [128, 2, TB], BF16, "sq")
    ob = kb.sb(st, [128, 2, TB], BF16, "ob")
    scr = kb.sb(st, [128, TB], F32, "scr")
    rstd = kb.sb(st, [128, TB], F32, "rstd")
    GC = 2.0 * math.sqrt(2.0 / math.pi)
    for tb in range(NB):
        cs = slice(tb * TB, (tb + 1) * TB)
        for m in range(2):
            kb.stt([ub, vv, Y], [yb], yb[:, m, :], ub[:, m, cs], vv[:, V_S5D + m:V_S5D + m + 1], Y[:, m, cs], ALU.mult, ALU.add)
            kb.tt([yb], [wk], wk[:, m, :], yb[:, m, :], yb[:, m, :], ALU.mult)
            kb.ts([wk], [wk], wk[:, m, :], wk[:, m, :], 0.044715, ALU.mult, 1.0, ALU.add)
            kb.tt([wk, yb], [wk], wk[:, m, :], wk[:, m, :], yb[:, m, :], ALU.mult)
            kb.act([wk], [wk], wk[:, m, :], wk[:, m, :], AF.Sigmoid, scale=GC)
            kb.tt([wk, yb], [gef], gef[:, m, :], wk[:, m, :], yb[:, m, :], ALU.mult)
            kb.op("dve", [gef], [geb], lambda: nc.vector.tensor_copy(out=geb[:, m, :], in_=gef[:, m, :]))
        for oc in range(2):
            p = nps(4)
            kb.mmg(p, p[:], [(Wg[:, kc, oc * 128:(oc + 1) * 128], geb[:, kc, :]) for kc in range(2)], [Wg, geb])
            kb.act([p, vv], [sgl], sgl[:, oc, :], p[:], AF.Sigmoid, bias=vv[:, V_BGLU + oc:V_BGLU + oc + 1])
            kb.tt([sgl, gef], [sgl], sgl[:, oc, :], sgl[:, oc, :], gef[:, oc, :], ALU.mult)
        group_norm_store(st, sgl, [sgl[:, 0, :], sgl[:, 1, :]], vv, V_MIX, 0, cs, sq, scr, rstd, ob)


def emit_s5h(kb, nc, st, l, vv, cst, jrow, s5s_d, s5b_d, cst2_d, w_glu, uT, psp, nps, group_norm_store, onesb):
    ident = cst[:, 0:128]
    st_outer = st
    Wg = kb.sb(st_outer, [128, 2, 256], BF16, "Wg")
    for kc in range(2):
        kb.dma("pool", [], [Wg], Wg[:, kc, :], w_glu[l, kc * 128:(kc + 1) * 128, :])
    ub = kb.sb(st_outer, [128, 2, L], BF16, "ub")
    kb.dma("sp", [], [ub], ub[:], uT.rearrange("(m p) t -> p m t", p=128))
    selb = kb.sb(st_outer, [128, 8, 240], BF16, "selb")
    kb.dma("pool", [], [selb], selb[:], cst2_d[:, 0:1920].rearrange("p (a b) -> p a b", a=8))
    Mg = kb.sb(st_outer, [128, 16, 128], BF16, "Mg")
    WXp = kb.sb(st_outer, [128, 64, 128], BF16, "WXp")
    WIp = kb.sb(st_outer, [128, 64, 128], BF16, "WIp")
    kb.memset([WXp], WXp[:], 0.0)
    kb.memset([WIp], WIp[:], 0.0, e="pool")
    r8 = kb.sb(st_outer, [128, 16], F32, "r8")
    th8 = kb.sb(st_outer, [128, 16], F32, "th8")
    st = contextlib.ExitStack()
    c2 = kb.sb(st, [128, 288], F32, "c2")
    kb.dma("sp", [], [c2], c2[:], cst2_d[:, 1920:2208])
    maskL, maskU = c2[:, 0:128], c2[:, 128:256]
    pss = kb.sb(st, [128, 3, 16], F32, "s5s")
    kb.dma("sp", [], [pss], pss[:], s5s_d[l].rearrange("p (a b) -> p a b", a=3))
    BC = kb.sb(st, [128, 4, 16, 16], F32, "BC")
    kb.dma("sp", [], [BC], BC[:], s5b_d[l].rearrange("p (a b c) -> p a b c", a=4, b=16))

    def tmp(n, nm):
        return kb.sb(st, [128, n], F32, nm)

    n = 16
    dl, are, aim, mag, sn, cs_, k_t, r_t = [tmp(n, x) for x in ("dl", "are", "aim", "mag", "sn", "cs", "kt", "rt")]
    kb.act([pss], [dl], dl[:], pss[:, 2, :], AF.Exp)
    kb.tt([pss, dl], [are], are[:], pss[:, 0, :], dl[:], ALU.mult)
    kb.tt([pss, dl], [aim], aim[:], pss[:, 1, :], dl[:], ALU.mult)
    kb.act([are], [mag], mag[:], are[:], AF.Exp)
    sincos(kb, nc, aim, aim[:], n, sn, sn[:], cs_, cs_[:], k_t, r_t)
    ar1, ai, den, fr, fi, t1 = [tmp(n, x) for x in ("ar1", "ai", "den", "fr", "fi", "t1")]
    kb.tt([mag, cs_], [ar1], ar1[:], mag[:], cs_[:], ALU.mult)
    kb.ts([ar1], [ar1], ar1[:], ar1[:], -1.0, ALU.add)
    kb.tt([mag, sn], [ai], ai[:], mag[:], sn[:], ALU.mult)
    kb.tt([pss], [den], den[:], pss[:, 0, :], pss[:, 0, :], ALU.mult)
    kb.tt([pss], [t1], t1[:], pss[:, 1, :], pss[:, 1, :], ALU.mult)
    kb.tt([den, t1], [den], den[:], den[:], t1[:], ALU.add)
    kb.recip([den], [den], den[:], den[:])
    kb.tt([ar1, pss], [fr], fr[:], ar1[:], pss[:, 0, :], ALU.mult)
    kb.tt([ai, pss], [t1], t1[:], ai[:], pss[:, 1, :], ALU.mult)
    kb.tt([fr, t1], [fr], fr[:], fr[:], t1[:], ALU.add)
    kb.tt([fr, den], [fr], fr[:], fr[:], den[:], ALU.mult)
    kb.tt([ai, pss], [fi], fi[:], ai[:], pss[:, 0, :], ALU.mult)
    kb.tt([ar1, pss], [t1], t1[:], ar1[:], pss[:, 1, :], ALU.mult)
    kb.tt([fi, t1], [fi], fi[:], fi[:], t1[:], ALU.subtract)
    kb.tt([fi, den], [fi], fi[:], fi[:], den[:], ALU.mult)
    bbr = kb.sb(st, [128, 16, 16], F32, "bbr")
    bbi = kb.sb(st, [128, 16, 16], F32, "bbi")
    tb_ = kb.sb(st, [128, 16, 16], F32, "tb_")
    frb = fr[:].unsqueeze(2).to_broadcast([128, 16, 16])
    fib = fi[:].unsqueeze(2).to_broadcast([128, 16, 16])
    kb.tt([fr, BC], [bbr], bbr[:], BC[:, 0], frb, ALU.mult)
    kb.tt([fi, BC], [tb_], tb_[:], BC[:, 1], fib, ALU.mult)
    kb.tt([bbr, tb_], [bbr], bbr[:], bbr[:], tb_[:], ALU.subtract)
    kb.tt([fr, BC], [bbi], bbi[:], BC[:, 1], frb, ALU.mult)
    kb.tt([fi, BC], [tb_], tb_[:], BC[:, 0], fib, ALU.mult)
    kb.tt([bbi, tb_], [bbi], bbi[:], bbi[:], tb_[:], ALU.add)
    PW = {}
    for nm, off in (("asc", 256), ("desc", 272)):
        pre = kb.sb(st, [128, 16, 16], F32, "pw_re")
        pim = kb.sb(st, [128, 16, 16], F32, "pw_im")
        are_j = kb.sb(st, [128, 16, 16], F32, "are_j")
        aim_j = kb.sb(st, [128, 16, 16], F32, "aim_j")
        for dq in range(16):
            kb.ts([c2, are], [are_j], are_j[:, dq, :], c2[:, off:off + 16], are[:, dq:dq + 1], ALU.mult)
            kb.ts([c2, aim], [aim_j], aim_j[:, dq, :], c2[:, off:off + 16], aim[:, dq:dq + 1], ALU.mult)
        mg_ = tmp(256, "mgj")
        sj = tmp(256, "sj")
        cj = tmp(256, "cj")
        kj = tmp(256, "kj")
        rj = tmp(256, "rj")
        kb.act([are_j], [mg_], mg_[:], are_j[:].rearrange("p a b -> p (a b)"), AF.Exp)
        sincos(kb, nc, aim_j, aim_j[:].rearrange("p a b -> p (a b)"), 256, sj, sj[:], cj, cj[:], kj, rj)
        kb.tt([mg_, cj], [pre], pre[:].rearrange("p a b -> p (a b)"), mg_[:], cj[:], ALU.mult)
        kb.tt([mg_, sj], [pim], pim[:].rearrange("p a b -> p (a b)"), mg_[:], sj[:], ALU.mult)
        PW[nm] = (pre, pim)
    kb.ts([are], [r8], r8[:], are[:], 8.0, ALU.mult)
    kb.act([r8], [r8], r8[:], r8[:], AF.Exp)
    kb.ts([aim], [th8], th8[:], aim[:], 8.0, ALU.mult)

    def cprod(d, Zr, Zi, Zt, tab, j0, nm):
        Pr, Pi = PW[tab]
        Tr = kb.sb(st, [128, 8, 8, 16], F32, nm + "r")
        Ti = kb.sb(st, [128, 8, 8, 16], F32, nm + "i")
        Tt = kb.sb(st, [128, 8, 8, 16], F32, nm + "t")
        sh_ = [128, 8, 8, 16]
        zr = Zr[:, d * 8:(d + 1) * 8, :].unsqueeze(2).to_broadcast(sh_)
        zi = Zi[:, d * 8:(d + 1) * 8, :].unsqueeze(2).to_broadcast(sh_)
        pr = Pr[:, d * 8:(d + 1) * 8, j0:j0 + 8].unsqueeze(3).to_broadcast(sh_)
        pi_ = Pi[:, d * 8:(d + 1) * 8, j0:j0 + 8].unsqueeze(3).to_broadcast(sh_)
        kb.tt(Zt + [Pr], [Tr], Tr[:], zr, pr, ALU.mult)
        kb.tt(Zt + [Pi], [Tt], Tt[:], zi, pi_, ALU.mult)
        kb.tt([Tr, Tt], [Tr], Tr[:], Tr[:], Tt[:], ALU.subtract)
        kb.tt(Zt + [Pi], [Ti], Ti[:], zr, pi_, ALU.mult)
        kb.tt(Zt + [Pr], [Tt], Tt[:], zi, pr, ALU.mult)
        kb.tt([Ti, Tt], [Ti], Ti[:], Ti[:], Tt[:], ALU.add)
        return Tr, Ti

    Bz = (bbr, bbi, [bbr, bbi])
    Cr_, Ci_ = BC[:, 2], BC[:, 3]

    class _V:
        def __init__(self, ap):
            self.ap = ap

        def __getitem__(self, k):
            return self.ap[k]

    Cz = (_V(Cr_), _V(Ci_), [BC])
    prods = {}
    prods["L", 0] = cprod(0, *Bz, "desc", 8, "Lf")
    prods["R", 0] = cprod(0, *Cz, "asc", 7, "Rf")
    prods["WI", 0] = cprod(0, *Cz, "asc", 8, "WIf")
    prods["WX", 0] = cprod(0, *Bz, "desc", 1, "WXf")
    prods["L", 1] = cprod(1, *Bz, "asc", 7, "Lb")
    prods["R", 1] = cprod(1, *Cz, "desc", 8, "Rb")
    prods["WI", 1] = cprod(1, *Cz, "desc", 0, "WIb")
    prods["WX", 1] = prods["L", 1]
    pp_i = 0
    for d in range(2):
        Lr, Li = prods["L", d]
        nLi = kb.sb(st, [128, 8, 8, 16], F32, "nLi")
        kb.ts([Li], [nLi], nLi[:], Li[:], -1.0, ALU.mult)
        prods["nLi", d] = nLi
        WIr, WIi = prods["WI", d]
        WXr, WXi = prods["WX", d]
        for q in range(8):
            for gl in range(2):
                g = 2 * q + gl
                rows = slice(gl * 64, (gl + 1) * 64)
                for ri, src, sgn in ((0, WIr, 1.0), (1, WIi, -1.0)):
                    idx = (d * 16 + g) * 2 + ri
                    kb.act([src], [WIp], WIp[rows, idx, :], src[rows, q].rearrange("p a b -> p (a b)"), AF.Copy, scale=sgn)
            for ri, src in ((0, WXr), (1, WXi)):
                pt_ = psp[pp_i % 4]
                pp_i += 1
                kb.op("pe", [src, cst], [pt_], lambda: nc.tensor.transpose(
                    out=pt_[:, 0:128], in_=src[:, q].rearrange("p a b -> p (a b)"), identity=ident))
                for gl in range(2):
                    idx = (d * 16 + 2 * q + gl) * 2 + ri
                    kb.act([pt_], [WXp], WXp[:, idx, gl * 64:(gl + 1) * 64], pt_[:, gl * 64:(gl + 1) * 64], AF.Copy)
    for g in range(16):
        q, gl = g // 2, g % 2
        rows = slice(gl * 64, (gl + 1) * 64)
        pm = []
        for d in range(2):
            Lr, Li = prods["L", d]
            nLi = prods["nLi", d]
            Rr, Ri = prods["R", d]
            pt_ = psp[4 + (pp_i % 4)]
            pp_i += 1
            f2 = lambda t_: t_[rows, q].rearrange("p a b -> p (a b)")
            kb.mmg(pt_, pt_[:, 0:128], [(f2(Lr), f2(Rr)), (f2(nLi), f2(Ri))], [Lr, nLi, Rr, Ri])
            pm.append(pt_)
        tm = kb.sb(st, [128, 128], F32, "tm")
        kb.tt([pm[0], c2], [tm], tm[:], pm[0][:, 0:128], maskL, ALU.mult)
        tm2 = kb.sb(st, [128, 128], F32, "tm2")
        kb.tt([pm[1], c2], [tm2], tm2[:], pm[1][:, 0:128], maskU, ALU.mult)
        kb.tt([tm, tm2], [Mg], Mg[:, g, :], tm[:], tm2[:], ALU.add)
    kb.phase_end()
    st.close()
    st = contextlib.ExitStack()
    Ug = kb.sb(st, [128, 16, TB], BF16, "Ug")
    Yg = kb.sb(st, [128, 16, TB], BF16, "Yg")
    Sp = kb.sb(st, [128, 32, TB], BF16, "Sp")
    tc_ = kb.sb(st, [128, TB], F32, "tcos")
    ts_ = kb.sb(st, [128, TB], F32, "tsin")
    argt = kb.sb(st, [128, TB], F32, "argt")
    kk = kb.sb(st, [128, TB], F32, "kk")
    rr = kb.sb(st, [128, TB], F32, "rr")
    w1 = [kb.sb(st, [128, TB], F32, "w1") for _ in range(2)]
    w2 = [kb.sb(st, [128, TB], F32, "w2") for _ in range(2)]
    xr_ = [kb.sb(st, [128, TB], F32, "xpr") for _ in range(2)]
    xi_ = [kb.sb(st, [128, TB], F32, "xpi") for _ in range(2)]
    Gr = [kb.sb(st, [128, TB], F32, "Gr") for _ in range(2)]
    Gi = [kb.sb(st, [128, TB], F32, "Gi") for _ in range(2)]
    for g in range(16):
        m, g8 = g // 8, g % 8
        pu = psp[g % 4]
        kb.mmg(pu, pu[:], [(selb[:, g8, (7 - k) * 16:(7 - k) * 16 + 128], ub[:, m, k:L:8]) for k in range(8)], [selb, ub])
        if g % 2 == 0:
            kb.act([pu], [Ug], Ug[:, g, :], pu[:], AF.Copy)
        else:
            kb.op("dve", [pu], [Ug], lambda: nc.vector.tensor_copy(out=Ug[:, g, :], in_=pu[:]))
    it = 0
    for d in range(2):
        def rv(ap):
            return ap if d == 0 else ap[:, ::-1]
        for q in range(8):
            dq = d * 8 + q
            kb.ts([cst, th8], [argt], argt[:], jrow, th8[:, dq:dq + 1], ALU.mult)
            sincos(kb, nc, argt, argt[:], TB, ts_, ts_[:], tc_, tc_[:], kk, rr)
            px = []
            for ri in range(2):
                p_ = psp[(2 * it + ri) % 4]
                kb.mmg(p_, p_[:], [(WXp[:, (d * 16 + 2 * q + gl) * 2 + ri, :], Ug[:, 2 * q + gl, :]) for gl in range(2)], [WXp, Ug])
                px.append(p_)
            pr_, pi_ = px
            a1, a2 = w1[it % 2], w2[it % 2]
            xr, xi = xr_[it % 2], xi_[it % 2]
            gr, gi = Gr[it % 2], Gi[it % 2]
            C, S = tc_[:], ts_[:]
            kb.tt([pr_, tc_], [a1], a1[:], rv(pr_[:]), C, ALU.mult)
            kb.tt([pi_, ts_], [a2], a2[:], rv(pi_[:]), S, ALU.mult)
            kb.tt([a1, a2], [xr], xr[:], a1[:], a2[:], ALU.add)
            kb.tt([pi_, tc_], [a1], a1[:], rv(pi_[:]), C, ALU.mult)
            kb.tt([pr_, ts_], [a2], a2[:], rv(pr_[:]), S, ALU.mult)
            kb.tt([a1, a2], [xi], xi[:], a1[:], a2[:], ALU.subtract)
            dec = r8[:, dq:dq + 1].to_broadcast([128, TB])
            kb.op("dve", [r8, xr], [gr], lambda: nc.vector.tensor_tensor_scan(
                out=gr[:], data0=dec, data1=xr[:], initial=0.0, op0=ALU.mult, op1=ALU.add))
            kb.op("dve", [r8, xi], [gi], lambda: nc.vector.tensor_tensor_scan(
                out=gi[:], data0=dec, data1=xi[:], initial=0.0, op0=ALU.mult, op1=ALU.add))
            n1 = TB - 1
            for ri in range(2):
                o_ = rv(Sp[:, dq * 2 + ri, :])
                kb.memset([Sp], o_[:, 0:1], 0.0, e="pool")
                if ri == 0:
                    kb.tt([gr, tc_], [a1], a1[:, 0:n1], gr[:, 0:n1], C[:, 0:n1], ALU.mult)
                    kb.tt([gi, ts_], [a2], a2[:, 0:n1], gi[:, 0:n1], S[:, 0:n1], ALU.mult)
                    kb.tt([a1, a2], [Sp], o_[:, 1:TB], a1[:, 0:n1], a2[:, 0:n1], ALU.subtract)
                else:
                    kb.tt([gr, ts_], [a1], a1[:, 0:n1], gr[:, 0:n1], S[:, 0:n1], ALU.mult)
                    kb.tt([gi, tc_], [a2], a2[:, 0:n1], gi[:, 0:n1], C[:, 0:n1], ALU.mult)
                    kb.tt([a1, a2], [Sp], o_[:, 1:TB], a1[:, 0:n1], a2[:, 0:n1], ALU.add)
            it += 1
    for g in range(16):
        q = g // 2
        py = psp[4 + (g % 4)]
        terms = [(Mg[:, g, :], Ug[:, g, :])]
        for d in range(2):
            for ri in range(2):
                terms.append((WIp[:, (d * 16 + g) * 2 + ri, :], Sp[:, (d * 8 + q) * 2 + ri, :]))
        kb.mmg(py, py[:], terms, [Mg, Ug, WIp, Sp])
        if g % 2 == 0:
            kb.act([py], [Yg], Yg[:, g, :], py[:], AF.Copy)
        else:
            kb.op("dve", [py], [Yg], lambda: nc.vector.tensor_copy(out=Yg[:, g, :], in_=py[:]))
    yb = kb.sb(st, [128, 2, TB], F32, "yb")
    wk = kb.sb(st, [128, 2, TB], F32, "wk")
    geb = kb.sb(st, [128, 2, TB], BF16, "geb")
    gef = kb.sb(st, [128, 2, TB], F32, "gef")
    sgl = kb.sb(st, [128, 2, TB], F32, "sgl")
    sq = kb.sb(st, [128, 2, TB], BF16, "sq")
    ob = kb.sb(st, [128, 2, TB], BF16, "ob")
    scr = kb.sb(st, [128, TB], F32, "scr")
    rstd = kb.sb(st, [128, TB], F32, "rstd")
    GC = 2.0 * math.sqrt(2.0 / math.pi)
    for tb in range(NB):
        cs = slice(tb * TB, (tb + 1) * TB)
        for m in range(2):
            pyb = psp[(2 * tb + m) % 4]
            kb.wait("pe", kb.deps([selb, Yg], [pyb]))
            inst = None
            for t in range(8):
                for g8 in range(8):
                    inst = nc.tensor.matmul(pyb[:][:, t:TB:8], selb[:, t, (7 - g8) * 16:(7 - g8) * 16 + 128],
                                            Yg[:, 8 * m + g8, tb * 64:(tb + 1) * 64], start=(g8 == 0), stop=(g8 == 7))
            kb.cnt["pe"] += 1
            inst.then_inc(kb.sem["pe"], 1)
            tok = (kb.sem["pe"], kb.cnt["pe"])
            kb.fin(tok, [selb, Yg], [pyb])
            kb.stt([ub, vv, pyb], [yb], yb[:, m, :], ub[:, m, cs], vv[:, V_S5D + m:V_S5D + m + 1], pyb[:], ALU.mult, ALU.add)
            kb.tt([yb], [wk], wk[:, m, :], yb[:, m, :], yb[:, m, :], ALU.mult)
            kb.ts([wk], [wk], wk[:, m, :], wk[:, m, :], 0.044715, ALU.mult, 1.0, ALU.add)
            kb.tt([wk, yb], [wk], wk[:, m, :], wk[:, m, :], yb[:, m, :], ALU.mult)
            kb.act([wk], [wk], wk[:, m, :], wk[:, m, :], AF.Sigmoid, scale=GC)
            kb.tt([wk, yb], [gef], gef[:, m, :], wk[:, m, :], yb[:, m, :], ALU.mult)
            kb.op("dve", [gef], [geb], lambda: nc.vector.tensor_copy(out=geb[:, m, :], in_=gef[:, m, :]))
        for oc in range(2):
            p = psp[4 + (oc % 2)]
            kb.mmg(p, p[:], [(Wg[:, kc, oc * 128:(oc + 1) * 128], geb[:, kc, :]) for kc in range(2)], [Wg, geb])
            kb.act([p, vv], [sgl], sgl[:, oc, :], p[:], AF.Sigmoid, bias=vv[:, V_BGLU + oc:V_BGLU + oc + 1])
            kb.tt([sgl, gef], [sgl], sgl[:, oc, :], sgl[:, oc, :], gef[:, oc, :], ALU.mult)
        group_norm_store(st, sgl, [sgl[:, 0, :], sgl[:, 1, :]], vv, V_MIX, 0, cs, sq, scr, rstd, ob)
    kb.phase_end()
    st.close()


_NC_CACHE = {}


def kernel(**inputs):
    inp = {k: np.asarray(v) for k, v in inputs.items()}
    sh, xs = host_prep(inp)
    if "nc" not in _NC_CACHE:
        _NC_CACHE["nc"] = build()
    nc = _NC_CACHE["nc"]
    n = len(xs)
    in_maps = []
    for b in range(n):
        m = dict(sh)
        m["xT"] = xs[b]
        in_maps.append(m)
    res = run_bass_kernel_spmd(nc, in_maps, core_ids=list(range(n)))
    out = np.stack([np.ascontiguousarray(res.results[b]["outT"].T) for b in range(n)])
    return out.astype(np.float32)
```
